# Optimizing a Trainium2 kernel written in Bass

```python
import math
import jax, jax.numpy as jnp
from jax import lax
import numpy as np

D_MODEL = 2048
BATCH = 16
SEQ = 2048
DEPTH = 2

HEAD_DIM = 128
N_HEADS = D_MODEL // HEAD_DIM
MIX_WIDTH = N_HEADS * HEAD_DIM
A_HEADS = N_HEADS // 2
A_KV_HEADS = max(1, A_HEADS // 4)
A_GROUP = A_HEADS // A_KV_HEADS
A_Q = A_HEADS * HEAD_DIM
A_KV = A_KV_HEADS * HEAD_DIM
B_HEADS = N_HEADS - A_HEADS
B_QK_DIM = HEAD_DIM // 2
B_V_DIM = HEAD_DIM
B_QK2 = B_HEADS * 2 * B_QK_DIM
B_V = B_HEADS * B_V_DIM
GATE_EVEN = A_Q + B_V
IN_EVEN = A_Q + 2 * A_KV + 2 * B_QK2 + B_V + GATE_EVEN
C_HEADS = N_HEADS
C_WIDTH = C_HEADS * HEAD_DIM
C_PATTERNS = ((128, 1), (512, 4), (2048, 16))
IN_ODD = 4 * C_WIDTH
GRID_W = 64
ROPE_THETA = 10000.0
ROPE_PAIRS = HEAD_DIM // 4
Q_BLOCK = 128
NORM_EPS = 1e-6
SUBLN_EPS = 1e-5
NEG_INF = -1e30

kernel_name = 'hybrid_gqa_diff_dilated_encoder'


def rms_norm(x, g, eps=NORM_EPS):
    xf = x.astype(jnp.float32)
    y = xf * lax.rsqrt(jnp.mean(xf * xf, axis=-1, keepdims=True) + eps)
    return (y * g.astype(jnp.float32)).astype(x.dtype)


def alibi_slopes(n):
    return jnp.exp2(-8.0 * jnp.arange(1, n + 1, dtype=jnp.float32) / n)


def axial_rope_tables(s):
    rows = s // GRID_W
    row_ids = jnp.broadcast_to(jnp.arange(rows)[:, None], (rows, GRID_W)).reshape(s).astype(jnp.float32)
    col_ids = jnp.broadcast_to(jnp.arange(GRID_W)[None, :], (rows, GRID_W)).reshape(s).astype(jnp.float32)
    inv_freq = ROPE_THETA ** (-jnp.arange(ROPE_PAIRS, dtype=jnp.float32) / ROPE_PAIRS)
    ang_r = row_ids[:, None] * inv_freq[None, :]
    ang_c = col_ids[:, None] * inv_freq[None, :]
    return jnp.cos(ang_r), jnp.sin(ang_r), jnp.cos(ang_c), jnp.sin(ang_c)


def _rope_section(xs, cos, sin):
    x1, x2 = xs[..., :ROPE_PAIRS], xs[..., ROPE_PAIRS:]
    c = cos[None, :, None, :]
    sn = sin[None, :, None, :]
    return jnp.concatenate([x1 * c - x2 * sn, x2 * c + x1 * sn], axis=-1)


def apply_axial_rope(x, tables):
    cr, sr, cc, sc = tables
    xf = x.astype(jnp.float32)
    half = HEAD_DIM // 2
    out = jnp.concatenate([_rope_section(xf[..., :half], cr, sr),
                           _rope_section(xf[..., half:], cc, sc)], axis=-1)
    return out.astype(x.dtype)


def gqa_attention(q, k, v):
    b, s, _, d = q.shape
    nb = s // Q_BLOCK
    scale = 1.0 / math.sqrt(d)
    qb = q.reshape(b, nb, Q_BLOCK, A_KV_HEADS, A_GROUP, d).transpose(1, 0, 2, 3, 4, 5)

    def block(qi):
        sc = jnp.einsum('bqkgd,bskd->bkgqs', qi, k).astype(jnp.float32) * scale
        p = jax.nn.softmax(sc, axis=-1)
        return jnp.einsum('bkgqs,bskd->bqkgd', p.astype(v.dtype), v)

    o = lax.map(block, qb)
    return o.transpose(1, 0, 2, 3, 4, 5).reshape(b, s, A_HEADS * d)


def diff_attention(q1, q2, k1, k2, v, lam, slopes):
    b, s, h, dq = q1.shape
    nb = s // Q_BLOCK
    scale = 1.0 / math.sqrt(dq)
    pos = jnp.arange(s)
    q1b = q1.reshape(b, nb, Q_BLOCK, h, dq).transpose(1, 0, 2, 3, 4)
    q2b = q2.reshape(b, nb, Q_BLOCK, h, dq).transpose(1, 0, 2, 3, 4)
    starts = jnp.arange(nb) * Q_BLOCK

    def block(args):
        q1i, q2i, t0 = args
        tq = t0 + jnp.arange(Q_BLOCK)
        dist = jnp.abs(tq[:, None] - pos[None, :]).astype(jnp.float32)
        bias = -slopes[:, None, None] * dist[None]
        s1 = jnp.einsum('bqhd,bshd->bhqs', q1i, k1).astype(jnp.float32) * scale + bias
        s2 = jnp.einsum('bqhd,bshd->bhqs', q2i, k2).astype(jnp.float32) * scale + bias
        p = jax.nn.softmax(s1, axis=-1) - lam * jax.nn.softmax(s2, axis=-1)
        return jnp.einsum('bhqs,bshd->bqhd', p.astype(v.dtype), v)

    o = lax.map(block, (q1b, q2b, starts))
    return o.transpose(1, 0, 2, 3, 4).reshape(b, s, h, v.shape[-1])


def dilated_window_attention(q, k, v, r, half, slopes):
    b, s, h, d = q.shape
    L = s // r
    nb = -(-L // half)
    lp = nb * half
    scale = 1.0 / math.sqrt(d)

    def sub(t):
        return t.reshape(b, L, r, h, d)

    qs = jnp.pad(sub(q), ((0, 0), (0, lp - L), (0, 0), (0, 0), (0, 0))).reshape(b, nb, half, r, h, d)
    pad_k = ((0, 0), (half, lp - L + half), (0, 0), (0, 0), (0, 0))
    kp = jnp.pad(sub(k), pad_k)
    vp = jnp.pad(sub(v), pad_k)
    win = jnp.arange(nb)[:, None] * half + jnp.arange(3 * half)[None, :]
    kb = jnp.take(kp, win, axis=1)
    vb = jnp.take(vp, win, axis=1)
    mq = jnp.arange(nb)[:, None] * half + jnp.arange(half)[None, :]
    mk = win - half
    rel = mk[:, None, :] - mq[:, :, None]
    valid = (jnp.abs(rel) <= half) & (mk[:, None, :] >= 0) & (mk[:, None, :] < L)
    dist = (jnp.abs(rel) * r).astype(jnp.float32)
    bias = -slopes[None, :, None, None] * dist[:, None]
    sc = jnp.einsum('bnqchd,bnkchd->bnchqk', qs, kb).astype(jnp.float32) * scale + bias[None, :, None]
    sc = jnp.where(valid[None, :, None, None], sc, NEG_INF)
    m = jnp.max(sc, axis=-1, keepdims=True)
    e = jnp.exp(sc - m)
    den = jnp.sum(e, axis=-1, keepdims=True)
    o = jnp.einsum('bnchqk,bnkchd->bnqchd', (e / den).astype(v.dtype), vb)
    lse = (m + jnp.log(den))[..., 0]
    o = o.reshape(b, lp, r, h, d)[:, :L].reshape(b, s, h, d)
    lse = lse.transpose(0, 1, 4, 2, 3).reshape(b, lp, r, h)[:, :L].reshape(b, s, h)
    return o, lse


def setup_inputs(seed: int = 0) -> dict:
    key = jax.random.key(seed)
    ks = jax.random.split(key, 16)
    n_even = (DEPTH + 1) // 2
    n_odd = DEPTH // 2
    f32 = jnp.float32

    def gain(k, shape):
        return 1.0 + 0.02 * jax.random.normal(k, shape, f32)

    return {
        'x': jax.random.normal(ks[0], (BATCH, SEQ, D_MODEL), f32),
        'ln_even_g': gain(ks[1], (n_even, D_MODEL)),
        'w_in_even': jax.random.normal(ks[2], (n_even, D_MODEL, IN_EVEN), f32) * D_MODEL ** -0.5,
        'a_q_norm_g': gain(ks[3], (n_even, HEAD_DIM)),
        'a_k_norm_g': gain(ks[4], (n_even, HEAD_DIM)),
        'b_lambda_q1': 0.1 * jax.random.normal(ks[5], (n_even, B_QK_DIM), f32),
        'b_lambda_k1': 0.1 * jax.random.normal(ks[6], (n_even, B_QK_DIM), f32),
        'b_lambda_q2': 0.1 * jax.random.normal(ks[7], (n_even, B_QK_DIM), f32),
        'b_lambda_k2': 0.1 * jax.random.normal(ks[8], (n_even, B_QK_DIM), f32),
        'b_subln_g': gain(ks[9], (n_even, B_V_DIM)),
        'w_out_even': jax.random.normal(ks[10], (n_even, MIX_WIDTH, D_MODEL), f32) * MIX_WIDTH ** -0.5,
        'ln_odd_g': gain(ks[11], (n_odd, D_MODEL)),
        'w_in_odd': jax.random.normal(ks[12], (n_odd, D_MODEL, IN_ODD), f32) * D_MODEL ** -0.5,
        'w_out_odd': jax.random.normal(ks[13], (n_odd, C_WIDTH, D_MODEL), f32) * C_WIDTH ** -0.5,
        'final_norm_g': gain(ks[14], (D_MODEL,)),
    }


def reference(x, ln_even_g, w_in_even, a_q_norm_g, a_k_norm_g, b_lambda_q1, b_lambda_k1,
              b_lambda_q2, b_lambda_k2, b_subln_g, w_out_even, ln_odd_g, w_in_odd, w_out_odd,
              final_norm_g):
    b, s, _ = x.shape
    rope_tables = axial_rope_tables(s)
    slopes_b = alibi_slopes(B_HEADS)
    slopes_c = alibi_slopes(C_HEADS)
    even_splits = np.cumsum([A_Q, A_KV, A_KV, B_QK2, B_QK2, B_V]).tolist()
    odd_splits = [C_WIDTH, 2 * C_WIDTH, 3 * C_WIDTH]
    h = x
    for layer in range(DEPTH):
        i = layer // 2
        if layer % 2 == 0:
            u = rms_norm(h, ln_even_g[i])
            proj = jnp.einsum('bsd,de->bse', u, w_in_even[i])
            qa, ka, va, qb, kb, vb, gate = jnp.split(proj, even_splits, axis=-1)
            qa = apply_axial_rope(rms_norm(qa.reshape(b, s, A_HEADS, HEAD_DIM), a_q_norm_g[i]), rope_tables)
            ka = apply_axial_rope(rms_norm(ka.reshape(b, s, A_KV_HEADS, HEAD_DIM), a_k_norm_g[i]), rope_tables)
            va = va.reshape(b, s, A_KV_HEADS, HEAD_DIM)
            ya = gqa_attention(qa, ka, va)
            qb = qb.reshape(b, s, B_HEADS, 2, B_QK_DIM)
            kb = kb.reshape(b, s, B_HEADS, 2, B_QK_DIM)
            vb = vb.reshape(b, s, B_HEADS, B_V_DIM)
            lambda_init = 0.8 - 0.6 * math.exp(-0.3 * layer)
            lam = (jnp.exp(jnp.sum(b_lambda_q1[i].astype(jnp.float32) * b_lambda_k1[i].astype(jnp.float32)))
                   - jnp.exp(jnp.sum(b_lambda_q2[i].astype(jnp.float32) * b_lambda_k2[i].astype(jnp.float32)))
                   + lambda_init)
            yb = diff_attention(qb[:, :, :, 0], qb[:, :, :, 1], kb[:, :, :, 0], kb[:, :, :, 1], vb, lam, slopes_b)
            yb = (rms_norm(yb, b_subln_g[i], SUBLN_EPS) * (1.0 - lambda_init)).reshape(b, s, B_V)
            y = jnp.concatenate([ya, yb], axis=-1) * jax.nn.silu(gate)
            h = h + jnp.einsum('bse,ed->bsd', y, w_out_even[i])
        else:
            u = rms_norm(h, ln_odd_g[i])
            proj = jnp.einsum('bsd,de->bse', u, w_in_odd[i])
            qc, kc, vc, gate = jnp.split(proj, odd_splits, axis=-1)
            qc = qc.reshape(b, s, C_HEADS, HEAD_DIM)
            kc = kc.reshape(b, s, C_HEADS, HEAD_DIM)
            vc = vc.reshape(b, s, C_HEADS, HEAD_DIM)
            outs, lses = [], []
            for window, dil in C_PATTERNS:
                o, l = dilated_window_attention(qc, kc, vc, dil, window // (2 * dil), slopes_c)
                outs.append(o)
                lses.append(l)
            wts = jax.nn.softmax(jnp.stack(lses, axis=0), axis=0)
            yc = jnp.sum(wts[..., None].astype(vc.dtype) * jnp.stack(outs, axis=0), axis=0)
            y = yc.reshape(b, s, C_WIDTH) * jax.nn.silu(gate)
            h = h + jnp.einsum('bse,ed->bsd', y, w_out_odd[i])
    return rms_norm(h, final_norm_g)
```

```python
import math
from contextlib import ExitStack

import numpy as np
import concourse.bass as bass
import concourse.mybir as mybir
from concourse.bass_utils import run_bass_kernel_spmd

F32 = mybir.dt.float32
BF16 = mybir.dt.bfloat16
I16 = mybir.dt.int16
AF = mybir.ActivationFunctionType
ALU = mybir.AluOpType
AX = mybir.AxisListType

NORM_EPS = 1e-6
SUBLN_EPS = 1e-5
N_CORES = 8


class Cfg:
    def __init__(self, D=2048, S=2048, NSEQ=2):
        self.D, self.S, self.NSEQ = D, S, NSEQ
        self.KC = D // 128
        self.NT = S // 128
        self.QG = min(512, S)
        self.NQG = S // self.QG
        self.NH = D // 128
        self.AH = self.NH // 2
        self.AKV = max(1, self.AH // 4)
        self.AG = self.AH // self.AKV
        self.BH = self.NH - self.AH
        self.CH = self.NH
        self.oqa = 0
        self.oka = self.AH * 128
        self.ova = self.oka + self.AKV * 128
        self.oqb = self.ova + self.AKV * 128
        self.okb = self.oqb + self.BH * 128
        self.ovb = self.okb + self.BH * 128
        self.ogate = self.ovb + self.BH * 128
        self.IN_EVEN = self.ogate + self.NH * 128
        self.IN_ODD = 4 * D
        self.J0 = S - 128
        self.W = 2 * S - 128
        self.NOC = D // 512


class Region:
    def __init__(self, name):
        self.name = name
        self.cur = None
        self.toks = {}
        self.barrier = {}


class Res:
    def __init__(self, name, region=None, gen=None):
        self.name = name
        self.w = {}
        self.r = {}
        self.region = region
        self.gen = gen
        self.dsem = None
        self.dval = 0


def _upd(d, k, sem, val):
    if k not in d or d[k][1] < val:
        d[k] = (sem, val)


class RecIns:
    def __init__(self, entry):
        self.entry = entry

    def then_inc(self, sem, n):
        self.entry[3] = (sem, n)
        return self


class Rec:
    def __init__(self):
        self.prog = []

    def __getattr__(self, name):
        def f(*a, **kw):
            entry = [name, a, kw, None]
            self.prog.append(entry)
            return RecIns(entry)
        return f

    def replay(self, e):
        for name, a, kw, inc in self.prog:
            ins = getattr(e, name)(*a, **kw)
            if inc is not None:
                ins.then_inc(*inc)


class Eng:
    def __init__(self, key, eng, sem, selfsync):
        self.key, self.real, self.sem, self.selfsync = key, eng, sem, selfsync
        self.eng = Rec()
        self.cnt = 0
        self.waited = {}
        self.nwaits = 0
        self.nins = 0

    def _need(self, reads, writes, extra=()):
        need = {}

        def add(d, allow_self):
            for k, (sem, val) in d.items():
                if k == self.key and not allow_self:
                    continue
                if k not in need or need[k][1] < val:
                    need[k] = (sem, val)

        for r in reads:
            if r.region is not None:
                assert r.region.cur == r.gen, f"read of {r.name} while region holds {r.region.cur}"
            add(r.w, self.selfsync)
        for w in writes:
            if w.region is not None:
                if w.region.cur != w.gen:
                    w.region.barrier = dict(w.region.toks)
                    w.region.cur = w.gen
                add(w.region.barrier, self.selfsync)
            add(w.w, self.selfsync)
            add(w.r, self.selfsync)
        for k, sem, val in extra:
            if k not in need or need[k][1] < val:
                need[k] = (sem, val)
        for k, (sem, val) in need.items():
            if self.waited.get(k, 0) < val:
                self.eng.wait_ge(sem, val)
                self.waited[k] = val
                self.nwaits += 1

    def _mark(self, k, sem, val, reads, writes):
        for r in reads:
            _upd(r.r, k, sem, val)
            if r.region is not None:
                _upd(r.region.toks, k, sem, val)
        for w in writes:
            _upd(w.w, k, sem, val)
            if w.region is not None:
                _upd(w.region.toks, k, sem, val)

    def op(self, fn, reads=(), writes=(), inc=True):
        psr = [r for r in reads if r.name.startswith("ps")]
        if psr:
            reads = [r for r in reads if not r.name.startswith("ps")]
            writes = list(writes) + psr
        self._need(reads, writes)
        ins = fn(self.eng)
        self.nins += 1
        if inc:
            ins.then_inc(self.sem, 1)
            self.cnt += 1
            val = self.cnt
        else:
            val = self.cnt + 1
        self._mark(self.key, self.sem, val, reads, writes)
        return ins

    def dma(self, K, out, in_, reads, writes, sres, chain=False):
        if sres.dsem is None:
            sres.dsem = K.new_sem("d_" + sres.name)
        k = "d_" + sres.name
        extra = ()
        if sres.dval > 0 and not chain:
            extra = ((k, sres.dsem, sres.dval),)
        self._need(reads, writes, extra)
        ins = self.eng.dma_start(out=out, in_=in_)
        ins.then_inc(sres.dsem, 16)
        self.nins += 1
        sres.dval += 16
        self._mark(k, sres.dsem, sres.dval, reads, writes)
        return ins


class K:
    def __init__(self, nc, es):
        self.nc, self.es = nc, es
        self.nsem = 0

    def new_sem(self, name):
        self.nsem += 1
        return self.es.enter_context(self.nc.semaphore(name))

    def sb(self, name, shape, dt):
        return self.es.enter_context(self.nc.sbuf_tensor(name, shape, dt))

    def ps(self, name, shape, dt):
        return self.es.enter_context(self.nc.psum_tensor(name, shape, dt))


def weight_tiles(cfg):
    c = cfg
    ev = []
    ev.append(("Akv", [p for g in range(c.AKV) for p in ((c.oka + g * 128, 128), (c.ova + g * 128, 128))]))
    for g in range(c.AKV):
        ev.append((f"Aq{g}", [(c.oqa + g * c.AG * 128, c.AG * 128)]))
        ev.append((f"Ag{g}", [(c.ogate + g * c.AG * 128, c.AG * 128)]))
    for h in range(c.BH):
        ev.append((f"B{h}", [(c.oqb + h * 128, 128), (c.okb + h * 128, 128),
                             (c.ogate + (c.AH + h) * 128, 128), (c.ovb + h * 128, 128)]))
    od = []
    for h in range(c.CH):
        od.append((f"C{h}", [(h * 128, 128), (c.D + h * 128, 128), (3 * c.D + h * 128, 128), (2 * c.D + h * 128, 128)]))
    out = [(f"O{i}", [(i * 512, 512)]) for i in range(c.NOC)]
    return ev, od, out


def build(cfg):
    c = cfg
    D, S, KC, NT, QG, NQG, NSEQ = c.D, c.S, c.KC, c.NT, c.QG, c.NQG, c.NSEQ
    TOK = NSEQ * S
    nc = bass.Bass("TRN2", target_bir_lowering=False)

    def dram(name, shape, dt, kind):
        return nc.dram_tensor(name, shape, dt, kind=kind).ap()

    x = dram("x", [TOK, D], F32, "ExternalInput")
    ln_even_g = dram("ln_even_g", [D], F32, "ExternalInput")
    w_in_even = dram("w_in_even", [D, c.IN_EVEN], F32, "ExternalInput")
    a_q_norm_g = dram("a_q_norm_g", [128], F32, "ExternalInput")
    a_k_norm_g = dram("a_k_norm_g", [128], F32, "ExternalInput")
    lamv = [dram(n, [64], F32, "ExternalInput") for n in ("b_lambda_q1", "b_lambda_k1", "b_lambda_q2", "b_lambda_k2")]
    b_subln_g = dram("b_subln_g", [128], F32, "ExternalInput")
    w_out_even = dram("w_out_even", [D, D], F32, "ExternalInput")
    ln_odd_g = dram("ln_odd_g", [D], F32, "ExternalInput")
    w_in_odd = dram("w_in_odd", [D, c.IN_ODD], F32, "ExternalInput")
    w_out_odd = dram("w_out_odd", [D, D], F32, "ExternalInput")
    final_norm_g = dram("final_norm_g", [D], F32, "ExternalInput")
    c_ident = dram("c_ident", [128, 128], F32, "ExternalInput")
    c_ropeC = dram("c_ropeC", [S, 128], F32, "ExternalInput")
    c_ropeS = dram("c_ropeS", [S, 128], F32, "ExternalInput")
    c_dabs = dram("c_dabs", [128, c.W], I16, "ExternalInput")
    c_cmul = dram("c_cmul", [128, c.W], F32, "ExternalInput")
    out = dram("out", [TOK, D], F32, "ExternalOutput")

    ev_tiles, od_tiles, out_tiles = weight_tiles(c)
    wb_ev = dram("wb_ev", [len(ev_tiles), 128, KC, 512], BF16, "Internal")
    wb_oe = dram("wb_oe", [c.NOC, 128, KC, 512], BF16, "Internal")
    wb_od = dram("wb_od", [len(od_tiles), 128, KC, 512], BF16, "Internal")
    wb_oo = dram("wb_oo", [c.NOC, 128, KC, 512], BF16, "Internal")
    yT_d = dram("yT_d", [c.NH, 128, S], BF16, "Internal")
    h1_d = dram("h1_d", [TOK, D], F32, "Internal")
    h2_d = dram("h2_d", [TOK, D], F32, "Internal")

    es = ExitStack()
    with es:
        k = K(nc, es)
        U = k.sb("U", [128, KC, S], BF16)
        wsl = [k.sb(f"wsl{i}", [128, KC, 512], BF16) for i in range(2)]
        QT = k.sb("QT", [128, 4, S], BF16)
        KT = k.sb("KT", [128, 2, S], BF16)
        V = k.sb("V", [128, NT, 2, 130], BF16)
        sgT = k.sb("sgT", [128, 2, S], BF16)
        ybuf = k.sb("ybuf", [128, 1, S], BF16)
        pT = k.sb("pT", [128, 6, 512], BF16)
        ident = k.sb("ident", [128, 128], BF16)
        identf = k.sb("identf", [128, 128], F32)
        gq_b = k.sb("gq_b", [128, 128], F32)
        gk_b = k.sb("gk_b", [128, 128], F32)
        gsub_b = k.sb("gsub_b", [128, 128], F32)
        lamt = k.sb("lamt", [128, 4, 64], F32)
        lamj = k.sb("lamj", [128, 64], F32)
        st = k.sb("st", [128, 64], F32)
        XB = 53 * 1024 // 2
        X = k.sb("X", [128, XB], BF16)

        xoff = {}

        def xcarve(gen, name, nbytes, dt, shape_pat=None, **kw):
            off = xoff.get(gen, 0)
            assert off % 4 == 0
            xoff[gen] = off + ((nbytes + 3) // 4) * 4
            assert xoff[gen] <= XB * 2, (gen, name, xoff[gen])
            ap = X[:, off // 2:(off + nbytes) // 2]
            if dt != BF16:
                ap = ap.bitcast(dt)
            if shape_pat:
                ap = ap.rearrange(shape_pat, **kw)
            return ap

        regX = Region("X")
        regU = Region("U")

        n_hb = [xcarve("norm", f"hb{i}", D * 4, F32) for i in range(2)]
        n_ub = [xcarve("norm", f"ub{i}", D * 2, BF16) for i in range(2)]
        n_gb = xcarve("norm", "gb", D * 4, F32)
        n_junk = xcarve("norm", "junk", D * 2, BF16)
        n_ob = [xcarve("norm", f"ob{i}", D * 4, F32) for i in range(2)]
        r_hb = [Res(f"n_hb{i}", regX, "norm") for i in range(2)]
        r_ub = [Res(f"n_ub{i}", regX, "norm") for i in range(2)]
        r_gb = Res("n_gb", regX, "norm")
        r_junk = Res("n_junk", regX, "norm")
        r_ob = [Res(f"n_ob{i}", regX, "norm") for i in range(2)]
        a_dabs = xcarve("attn", "dabs", c.W * 2, I16)
        a_cmul = xcarve("attn", "cmul", c.W * 2, BF16)
        a_strip = [xcarve("attn", f"strip{i}", c.W * 2, BF16) for i in range(2)]
        a_ebuf = [xcarve("attn", f"ebuf{i}", 512 * 2, BF16) for i in range(3)]
        a_sqj = xcarve("attn", "sqj", 512 * 2, BF16, "p (h d) -> p h d", d=128)
        a_qn = [xcarve("attn", f"qn{i}", 512 * 4, F32, "p (h d) -> p h d", d=128) for i in range(2)]
        a_tb = [xcarve("attn", f"tb{i}", 512 * 4, F32, "p (h d) -> p h d", d=128) for i in range(2)]
        a_rot = [xcarve("attn", f"rot{i}", 512 * 2, BF16, "p (h d) -> p h d", d=128) for i in range(2)]
        a_rc = [xcarve("attn", f"rc{i}", 128 * 4, F32) for i in range(2)]
        a_rs = [xcarve("attn", f"rs{i}", 128 * 4, F32) for i in range(2)]
        a_o1 = [xcarve("attn", f"o1{i}", 256 * 4, F32, "p (h d) -> p h d", d=128) for i in range(2)]
        a_dd = [xcarve("attn", f"dd{i}", 256 * 4, F32, "p (h d) -> p h d", d=128) for i in range(2)]
        a_sqd = xcarve("attn", "sqd", 256 * 4, F32, "p (h d) -> p h d", d=128)
        a_yt = [xcarve("attn", f"yt{i}", 256 * 2, BF16, "p (h d) -> p h d", d=128) for i in range(2)]
        r_dabs = Res("a_dabs", regX, "attn")
        r_cmul = Res("a_cmul", regX, "attn")
        r_strip = [Res(f"a_strip{i}", regX, "attn") for i in range(2)]
        r_ebuf = [Res(f"a_ebuf{i}", regX, "attn") for i in range(3)]
        r_sqj = [Res(f"a_sqj{i}", regX, "attn") for i in range(4)]
        r_qn = [Res(f"a_qn{i}", regX, "attn") for i in range(2)]
        r_tb = [Res(f"a_tb{i}", regX, "attn") for i in range(2)]
        r_rot = [Res(f"a_rot{i}", regX, "attn") for i in range(2)]
        r_rc = [Res(f"a_rc{i}", regX, "attn") for i in range(2)]
        r_rs = [Res(f"a_rs{i}", regX, "attn") for i in range(2)]
        r_o1 = [Res(f"a_o1{i}", regX, "attn") for i in range(2)]
        r_dd = [Res(f"a_dd{i}", regX, "attn") for i in range(2)]
        r_sqd = Res("a_sqd", regX, "attn")
        r_yt = [Res(f"a_yt{i}", regX, "attn") for i in range(2)]
        o_hres = [xcarve("oproj", f"hres{i}", 512 * 4, F32) for i in range(2)]
        o_hout = [xcarve("oproj", f"hout{i}", 512 * 4, F32) for i in range(2)]
        r_hres = [Res(f"o_hres{i}", regX, "oproj") for i in range(2)]
        r_hout = [Res(f"o_hout{i}", regX, "oproj") for i in range(2)]

        r_uT = [Res(f"uT{t}", regU, "uT") for t in range(NT)]
        r_yT = [Res(f"yT{kc}", regU, "yT") for kc in range(KC)]

        r_wsl = [Res(f"wsl{i}") for i in range(2)]
        r_QT = [Res(f"QT{i}") for i in range(4)]
        r_KT = [Res(f"KT{i}") for i in range(2)]
        r_V = [Res(f"V{i}") for i in range(2)]
        r_sg = [Res(f"sg{i}") for i in range(2)]
        r_yb = [Res(f"yb{i}") for i in range(1)]
        r_pT = [Res(f"pT{i}") for i in range(6)]
        r_ident = Res("ident")
        r_identf = Res("identf")
        r_gq, r_gk, r_gsub, r_lamt, r_lamj = Res("gq"), Res("gk"), Res("gsub"), Res("lamt"), Res("lamj")
        st_names = ["ms0", "ms1", "ln0", "ln1", "rs0", "rs1", "qms0", "qln0", "qrs0", "qms1", "qln1", "qrs1", "rd0", "rd1", "rd2", "rd3",
                    "nl0", "nl1", "dms0", "dms1", "dln", "drs0", "drs1", "ls1", "ls2", "le1", "le2", "lam", "nlam"]
        st_w = {"qms0": 4, "qln0": 4, "qrs0": 4, "qms1": 4, "qln1": 4, "qrs1": 4, "rd0": 2, "rd1": 2, "rd2": 2, "rd3": 2, "nl0": 2, "nl1": 2,
                "dms0": 2, "dms1": 2, "dln": 2, "drs0": 2, "drs1": 2}
        stc, r_st = {}, {}
        _o = 0
        for n in st_names:
            wd = st_w.get(n, 1)
            stc[n] = st[:, _o:_o + wd]
            r_st[n] = Res("st_" + n)
            _o += wd
        assert _o <= 64

        psb = [k.ps(f"ps{i}", [128, 512], F32) for i in range(8)]
        r_ps = [Res(f"ps{i}") for i in range(8)]
        PS_S = [0, 1, 2, 3]
        PS_O = [[4, 5]]
        PS_M = [6, 7]
        LOOK = 3

        r_wev = [Res(f"wev{i}") for i in range(len(ev_tiles))]
        r_wod = [Res(f"wod{i}") for i in range(len(od_tiles))]
        r_woe = Res("woe")
        r_woo = Res("woo")
        r_yTd = [Res(f"yTd{h}") for h in range(c.NH)]
        r_h1 = [[Res(f"h1_{s}_{t}") for t in range(NT)] for s in range(NSEQ)]
        r_h2 = [[Res(f"h2_{s}_{t}") for t in range(NT)] for s in range(NSEQ)]
        r_out = Res("outd")
        r_in = Res("inputs")

        block = es.enter_context(nc.Block())
        PE = Eng("pe", nc.tensor, k.new_sem("s_pe"), False)
        ACT = Eng("act", nc.scalar, k.new_sem("s_act"), True)
        DVE = Eng("dve", nc.vector, k.new_sem("s_dve"), True)
        POOL = Eng("pool", nc.gpsimd, k.new_sem("s_pool"), True)
        SP = Eng("sp", nc.sync, k.new_sem("s_sp"), False)

        marks = []

        def mark(name):
            marks.append((name, PE.nins))

        cast_q = []

        def cast_tiles(src, dst, tiles, ress):
            for i, (nm, pieces) in enumerate(tiles):
                res = ress[i] if isinstance(ress, list) else ress
                cast_q.append((src, dst, i, pieces, res))

        def pump_casts(n):
            for _ in range(n):
                if not cast_q:
                    return
                src, dst, i, pieces, res = cast_q.pop(0)
                o = 0
                for (c0, wd) in pieces:
                    POOL.dma(k, out=dst[i, :, :, o:o + wd],
                             in_=src[:, c0:c0 + wd].rearrange("(kc p) c -> p kc c", p=128),
                             reads=[r_in], writes=[res], sres=res, chain=True)
                    o += wd

        wstate = {"n": 0}

        def load_w(dst_tiles, idx, res, ncols=512):
            while any((q[1] is dst_tiles and q[2] == idx) for q in cast_q):
                pump_casts(1)
            slot = wstate["n"] % 2
            wstate["n"] += 1
            SP.dma(k, out=wsl[slot][:, :, 0:ncols], in_=dst_tiles[idx, :, :, 0:ncols], reads=[res],
                   writes=[r_wsl[slot]], sres=r_wsl[slot])
            return slot

        mstate = {"n": 0}

        mpool = {"banks": list(range(8))}

        def mbank():
            bl = mpool["banks"]
            b = bl[mstate["n"] % len(bl)]
            mstate["n"] += 1
            return b

        evs = {"n": 0}

        def evac_eng():
            evs["n"] += 1
            return ACT if evs["n"] % 2 == 0 else DVE

        def copy_on(E, out_ap, in_ap, reads, writes):
            if E is ACT:
                E.op(lambda e: e.activation(out=out_ap, in_=in_ap, func=AF.Copy), reads, writes)
            else:
                E.op(lambda e: e.tensor_copy(out=out_ap, in_=in_ap), reads, writes)

        def rstd_from_ms(ms_ap, ln_ap, rs_ap, r_ms, r_ln, r_rs, eps):
            ACT.op(lambda e: e.activation(out=ln_ap, in_=ms_ap, func=AF.Ln, bias=eps_ap[eps], scale=1.0),
                   [r_ms, r_eps], [r_ln])
            ACT.op(lambda e: e.activation(out=rs_ap, in_=ln_ap, func=AF.Exp, scale=-0.5), [r_ln], [r_rs])

        epst = k.sb("epst", [128, 2], F32)
        r_eps = Res("eps")
        eps_ap = {NORM_EPS: epst[:, 0:1], SUBLN_EPS: epst[:, 1:2]}
        POOL.op(lambda e: e.memset(epst[:, 0:1], NORM_EPS), [], [r_eps])
        POOL.op(lambda e: e.memset(epst[:, 1:2], SUBLN_EPS), [], [r_eps])
        cast_tiles(w_in_even, wb_ev, ev_tiles, r_wev)
        cast_tiles(w_out_even, wb_oe, out_tiles, r_woe)
        cast_tiles(w_in_odd, wb_od, od_tiles, r_wod)
        cast_tiles(w_out_odd, wb_oo, out_tiles, r_woo)
        pump_casts(3 + 2 * (c.AKV - 1))

        SP.dma(k, out=identf[:], in_=c_ident, reads=[r_in], writes=[r_identf], sres=r_identf)
        DVE.op(lambda e: e.tensor_copy(out=ident[:], in_=identf[:]), [r_identf], [r_ident])
        SP.dma(k, out=gq_b[:], in_=a_q_norm_g.partition_broadcast(128), reads=[r_in], writes=[r_gq], sres=r_gq)
        SP.dma(k, out=gk_b[:], in_=a_k_norm_g.partition_broadcast(128), reads=[r_in], writes=[r_gk], sres=r_gk)
        SP.dma(k, out=gsub_b[:], in_=b_subln_g.partition_broadcast(128), reads=[r_in], writes=[r_gsub], sres=r_gsub)
        lambda_init = 0.8 - 0.6 * math.exp(-0.3 * 0)
        DVE.op(lambda e: e.tensor_scalar(out=gsub_b[:], in0=gsub_b[:], scalar1=0.5 * (1.0 - lambda_init), scalar2=None,
                                         op0=ALU.mult), [r_gsub], [r_gsub])
        for i in range(4):
            SP.dma(k, out=lamt[:, i, :], in_=lamv[i].partition_broadcast(128), reads=[r_in], writes=[r_lamt],
                   sres=r_lamt, chain=True)
        for (a, b, sname, ename) in ((0, 1, "ls1", "le1"), (2, 3, "ls2", "le2")):
            DVE.op(lambda e, a=a, b=b, sname=sname: e.tensor_tensor(out=lamj[:], in0=lamt[:, a, :], in1=lamt[:, b, :],
                                                                     op=ALU.mult), [r_lamt], [r_lamj])
            DVE.op(lambda e, sname=sname: e.tensor_reduce(out=stc[sname], in_=lamj[:], axis=AX.X, op=ALU.add),
                   [r_lamj], [r_st[sname]])
            ACT.op(lambda e, sname=sname, ename=ename: e.activation(out=stc[ename], in_=stc[sname], func=AF.Exp),
                   [r_st[sname]], [r_st[ename]])
        DVE.op(lambda e: e.tensor_tensor(out=stc["lam"], in0=stc["le1"], in1=stc["le2"], op=ALU.subtract),
               [r_st["le1"], r_st["le2"]], [r_st["lam"]])
        DVE.op(lambda e: e.tensor_scalar(out=stc["nlam"], in0=stc["lam"], scalar1=lambda_init, scalar2=-1.0,
                                         op0=ALU.add, op1=ALU.mult), [r_st["lam"]], [r_st["nlam"]])
        tbuf = k.sb("tbuf", [128, 512], F32)
        r_tbuf = Res("tbuf")
        nhalf = k.sb("nhalf", [128, 4], F32)
        r_nhalf = Res("nhalf")
        POOL.op(lambda e: e.memset(nhalf[:], -0.5), [], [r_nhalf])
        mk = k.sb("mk", [128, 2], F32)
        r_mk = Res("mk")
        POOL.op(lambda e: e.memset(mk[0:64, 0:1], 1.0), [], [r_mk])
        POOL.op(lambda e: e.memset(mk[64:128, 0:1], 0.0), [], [r_mk])
        POOL.op(lambda e: e.memset(mk[0:64, 1:2], 0.0), [], [r_mk])
        POOL.op(lambda e: e.memset(mk[64:128, 1:2], 1.0), [], [r_mk])
        POOL.op(lambda e: e.memset(V[:, :, :, 128:130], 1.0), [], [r_V[0], r_V[1]])

        def norm_phase(src, src_res, gvec, to_out=None, out_res=None, row0=0):
            SP.dma(k, out=n_gb, in_=gvec.partition_broadcast(128), reads=[r_in], writes=[r_gb], sres=r_gb)
            def stage_a(tt):
                i = tt % 2
                rows = slice(row0 + tt * 128, row0 + (tt + 1) * 128)
                SP.dma(k, out=n_hb[i], in_=src[rows, :], reads=[src_res[tt]], writes=[r_hb[i]], sres=r_hb[i])
                ms, ln, rs = stc[f"ms{i}"], stc[f"ln{i}"], stc[f"rs{i}"]
                ACT.op(lambda e: e.activation(out=n_junk, in_=n_hb[i], func=AF.Square,
                                              scale=float(D) ** -0.5, accum_out=ms),
                       [r_hb[i]], [r_junk, r_st[f"ms{i}"]])
                rstd_from_ms(ms, ln, rs, r_st[f"ms{i}"], r_st[f"ln{i}"], r_st[f"rs{i}"], NORM_EPS)

            def stage_a2(tt):
                i = tt % 2
                rs = stc[f"rs{i}"]
                dst, r_dst = (n_ub[i], r_ub[i]) if to_out is None else (n_ob[i], r_ob[i])
                DVE.op(lambda e: e.scalar_tensor_tensor(out=dst, in0=n_hb[i], scalar=rs, in1=n_gb,
                                                        op0=ALU.mult, op1=ALU.mult),
                       [r_hb[i], r_st[f"rs{i}"], r_gb], [r_dst])

            def stage_b(tt):
                i = tt % 2
                rows = slice(row0 + tt * 128, row0 + (tt + 1) * 128)
                if to_out is None:
                    for k4 in range(KC // 4):
                        b = mbank()
                        pv = psb[b][:].bitcast(BF16)
                        for j in range(4):
                            kc = k4 * 4 + j
                            PE.op(lambda e, pv=pv, j=j, kc=kc, i=i: e.transpose(
                                out=pv[:, j * 128:(j + 1) * 128], in_=n_ub[i][:, kc * 128:(kc + 1) * 128], identity=ident[:]),
                                [r_ub[i], r_ident], [r_ps[b]], inc=(j == 3))
                        E = evac_eng()
                        copy_on(E, U[:, k4 * 4:(k4 + 1) * 4, tt * 128:(tt + 1) * 128],
                                pv[:, 0:512].rearrange("p (a t) -> p a t", a=4), [r_ps[b]], [r_uT[tt]])
                else:
                    POOL.dma(k, out=to_out[rows, :], in_=n_ob[i], reads=[r_ob[i]], writes=[out_res], sres=r_ob[i])

            stage_a(0)
            stage_a2(0)
            for tt in range(NT):
                if tt + 1 < NT:
                    stage_a(tt + 1)
                stage_b(tt)
                if tt + 1 < NT:
                    stage_a2(tt + 1)

        fbstate = {"on": False, "n": 0, "held": None}

        def fbank():
            if not fbstate["on"]:
                return mbank()
            b = PS_M[fbstate["n"] % 2]
            fbstate["n"] += 1
            fbstate["held"] = b
            return b

        def frelease():
            fbstate["held"] = None

        def proj_fm_items(wslot, col0, dst3, dst_res, act_silu=False, evac=None):
            items = []
            for tg in range(NQG):
                def it(tg=tg):
                    if act_silu:
                        drain_pending()
                    b = fbank()
                    for kc in range(KC):
                        PE.op(lambda e, kc=kc: e.matmul(psb[b][:, 0:QG], lhsT=wsl[wslot][:, kc, col0:col0 + 128],
                                                        rhs=U[:, kc, tg * QG:(tg + 1) * QG],
                                                        start=(kc == 0), stop=(kc == KC - 1)),
                              [r_wsl[wslot]] + r_uT[tg * (QG // 128):(tg + 1) * (QG // 128)], [r_ps[b]], inc=(kc == KC - 1))
                        if kc < KC - 1:
                            yield
                    if evac is not None:
                        evac(b, tg)
                    else:
                        o = dst3[:, tg * QG:(tg + 1) * QG]
                        if act_silu:
                            ACT.op(lambda e: e.activation(out=tbuf[:, 0:QG], in_=psb[b][:, 0:QG], func=AF.Tanh, scale=0.5),
                                   [r_ps[b]], [r_tbuf])
                            DVE.op(lambda e: e.scalar_tensor_tensor(out=o, in0=tbuf[:, 0:QG], scalar=1.0, in1=psb[b][:, 0:QG],
                                                                    op0=ALU.add, op1=ALU.mult),
                                   [r_tbuf, r_ps[b]], [dst_res])
                        else:
                            copy_on(DVE, o, psb[b][:, 0:QG], [r_ps[b]], [dst_res])
                    frelease()
                    yield
                items.append(it)
            return items

        def proj_v_items(wslot, col0, vslot):
            items = []
            for tt in range(NT):
                def it(tt=tt):
                    b = fbank()
                    for kc in range(KC):
                        PE.op(lambda e, kc=kc: e.matmul(psb[b][:, 0:128], lhsT=U[:, kc, tt * 128:(tt + 1) * 128],
                                                        rhs=wsl[wslot][:, kc, col0:col0 + 128], start=(kc == 0),
                                                        stop=(kc == KC - 1)),
                              [r_wsl[wslot], r_uT[tt]], [r_ps[b]], inc=(kc == KC - 1))
                        if kc < KC - 1:
                            yield
                    copy_on(DVE, V[:, tt, vslot, 0:128], psb[b][:, 0:128], [r_ps[b]], [r_V[vslot]])
                    frelease()
                    yield
                items.append(it)
            return items

        def head_items(wslot, sl):
            its = proj_v_items(wslot, 384, sl)
            its += proj_fm_items(wslot, 0, QT[:, sl, :], r_QT[sl])
            its += proj_fm_items(wslot, 128, KT[:, sl, :], r_KT[sl])
            its += proj_fm_items(wslot, 256, sgT[:, sl, :], r_sg[sl], act_silu=True)
            return its

        def run_items(items):
            for it in items:
                for _ in it():
                    pass

        def load_rope(tt):
            i = tt % 2
            SP.dma(k, out=a_rc[i], in_=c_ropeC[tt * 128:(tt + 1) * 128, :], reads=[r_in], writes=[r_rc[i]], sres=r_rc[i])
            SP.dma(k, out=a_rs[i], in_=c_ropeS[tt * 128:(tt + 1) * 128, :], reads=[r_in], writes=[r_rs[i]], sres=r_rs[i])

        qstate = {"n": 0}

        def qk_post(b, c0, nh, gain, r_gain, tt):
            bi = qstate["n"] % 2
            qstate["n"] += 1
            i = tt % 2
            src = psb[b][:, c0:c0 + nh * 128].rearrange("p (h d) -> p h d", d=128)
            qms, qln, qrs = stc[f"qms{bi}"], stc[f"qln{bi}"], stc[f"qrs{bi}"]
            r_qms, r_qln, r_qrs = r_st[f"qms{bi}"], r_st[f"qln{bi}"], r_st[f"qrs{bi}"]
            for h in range(nh):
                ACT.op(lambda e, h=h: e.activation(out=a_sqj[:, h, :], in_=src[:, h, :], func=AF.Square,
                                                   scale=128.0 ** -0.5, accum_out=qms[:, h:h + 1]),
                       [r_ps[b]], [r_sqj[h], r_qms])
            rstd_from_ms(qms[:, 0:nh], qln[:, 0:nh], qrs[:, 0:nh], r_qms, r_qln, r_qrs, NORM_EPS)
            qn, tb, rot = a_qn[bi], a_tb[bi], a_rot[bi]
            for h in range(nh):
                DVE.op(lambda e, h=h: e.scalar_tensor_tensor(out=qn[:, h, :], in0=src[:, h, :],
                                                             scalar=qrs[:, h:h + 1], in1=gain[:],
                                                             op0=ALU.mult, op1=ALU.mult),
                       [r_ps[b], r_qrs, r_gain], [r_qn[bi]])
            qv = qn[:, 0:nh, :].rearrange("p h (s t d) -> p h s t d", s=2, t=2)
            tv = tb[:, 0:nh, :].rearrange("p h (s t d) -> p h s t d", s=2, t=2)
            sv = a_rs[i].rearrange("p (o s t d) -> p o s t d", o=1, s=2, t=2)
            for t in range(2):
                POOL.op(lambda e, t=t: e.tensor_tensor(out=tv[:, :, :, t, :], in0=qv[:, :, :, 1 - t, :],
                                                       in1=sv[:, :, :, t, :].broadcast_to([128, nh, 2, 32]), op=ALU.mult),
                        [r_qn[bi], r_rs[i]], [r_tb[bi]])
            cb = a_rc[i].rearrange("p (o d) -> p o d", o=1).broadcast_to([128, nh, 128])
            DVE.op(lambda e: e.tensor_tensor(out=qn[:, 0:nh, :], in0=qn[:, 0:nh, :], in1=cb, op=ALU.mult),
                   [r_qn[bi], r_rc[i]], [r_qn[bi]])
            DVE.op(lambda e: e.tensor_tensor(out=rot[:, 0:nh, :], in0=qn[:, 0:nh, :], in1=tb[:, 0:nh, :], op=ALU.add),
                   [r_qn[bi], r_tb[bi]], [r_rot[bi]])
            return bi

        def a_kv_phase(wslot):
            ncol = c.AKV * 256
            pend = []
            for tt in range(NT + 1):
                b = None
                if tt < NT:
                    load_rope(tt)
                    b = mbank()
                    for kc in range(KC):
                        PE.op(lambda e, kc=kc, b=b, tt=tt: e.matmul(psb[b][:, 0:ncol], lhsT=U[:, kc, tt * 128:(tt + 1) * 128],
                                                                     rhs=wsl[wslot][:, kc, 0:ncol], start=(kc == 0),
                                                                     stop=(kc == KC - 1)),
                              [r_wsl[wslot], r_uT[tt]], [r_ps[b]], inc=(kc == KC - 1))
                for (ptt, g, bi) in pend:
                    b2 = mbank()
                    pv = psb[b2][:].bitcast(BF16)
                    PE.op(lambda e, pv=pv, bi=bi: e.transpose(out=pv[:, 0:128], in_=a_rot[bi][:, 0, :], identity=ident[:]),
                          [r_rot[bi], r_ident], [r_ps[b2]])
                    copy_on(evac_eng(), KT[:, g, ptt * 128:(ptt + 1) * 128], pv[:, 0:128], [r_ps[b2]], [r_KT[g]])
                pend = []
                if tt < NT:
                    for g in range(c.AKV):
                        copy_on(evac_eng(), V[:, tt, g, 0:128], psb[b][:, g * 256 + 128:g * 256 + 256], [r_ps[b]], [r_V[g]])
                        bi = qk_post(b, g * 256, 1, gk_b, r_gk, tt)
                        pend.append((tt, g, bi))

        def a_q_phase(wslot):
            nh = c.AG
            pend = None
            for tt in range(NT + 1):
                b = None
                if tt < NT:
                    load_rope(tt)
                    b = mbank()
                    for kc in range(KC):
                        PE.op(lambda e, kc=kc, b=b, tt=tt: e.matmul(psb[b][:, 0:nh * 128], lhsT=U[:, kc, tt * 128:(tt + 1) * 128],
                                                                     rhs=wsl[wslot][:, kc, 0:nh * 128], start=(kc == 0),
                                                                     stop=(kc == KC - 1)),
                              [r_wsl[wslot], r_uT[tt]], [r_ps[b]], inc=(kc == KC - 1))
                if pend is not None:
                    ptt, bi = pend
                    b2 = mbank()
                    pv = psb[b2][:].bitcast(BF16)
                    for h in range(nh):
                        PE.op(lambda e, pv=pv, h=h, bi=bi: e.transpose(out=pv[:, h * 128:(h + 1) * 128], in_=a_rot[bi][:, h, :],
                                                                       identity=ident[:]),
                              [r_rot[bi], r_ident], [r_ps[b2]], inc=(h == nh - 1))
                    copy_on(evac_eng(), QT[:, 0:nh, ptt * 128:(ptt + 1) * 128],
                            pv[:, 0:nh * 128].rearrange("p (h t) -> p h t", h=nh), [r_ps[b2]], r_QT[0:nh])
                    pend = None
                if tt < NT:
                    bi = qk_post(b, 0, nh, gq_b, r_gq, tt)
                    pend = (tt, bi)

        ostate = {"n": 0}

        def attention(qslot, kslot, vslot, gslot, yslot, head_feat, kind, scale, strip_i=None, nmaps=1, filler=(),
                      k2slot=None):
            QB = QG // 128
            pump_casts(2)
            mpool["banks"] = list(PS_M)
            filler = list(filler)
            if kind == "C":
                tot_blocks = sum(1 for qg_ in range(NQG) for kb_ in range(NT)
                                 if any(abs(kb_ - (qg_ * QB + qs_)) <= 8 for qs_ in range(QB)))
            else:
                tot_blocks = NQG * NT * nmaps
            fill = {"done": 0, "blk": 0, "i": 0, "cur": None}
            tot_steps = len(filler) * KC
            fbstate["on"] = True

            def advance(n):
                for _ in range(n):
                    while True:
                        if fill["cur"] is None:
                            if fill["i"] >= len(filler):
                                return
                            fill["cur"] = filler[fill["i"]]()
                            fill["i"] += 1
                        try:
                            next(fill["cur"])
                            fill["done"] += 1
                            break
                        except StopIteration:
                            fill["cur"] = None

            def do_fill(extra=0):
                step_pending()
                fill["blk"] += 1
                target = (tot_steps * fill["blk"]) // tot_blocks + extra
                if target > fill["done"]:
                    advance(target - fill["done"])

            oset = PS_O[0]
            osets = [oset, oset]
            stream = []
            for qg_ in range(NQG):
                for m_ in range(nmaps):
                    blocks = []
                    for kb_ in range(NT):
                        act_qs = [qs for qs in range(QB) if (kind != "C" or abs(kb_ - (qg_ * QB + qs)) <= 8)]
                        if act_qs:
                            blocks.append((kb_, act_qs))
                    for bi_, (kb_, act_qs) in enumerate(blocks):
                        last_kb = {qs: max(b_[0] for b_ in blocks if qs in b_[1]) for qs in act_qs}
                        stream.append((qg_, m_, kb_, act_qs, last_kb, bi_ == 0, bi_ == len(blocks) - 1))
            NS = len(stream)
            sbase = sstate["n"]
            sstate["n"] += NS
            started = set()

            def emit_qk(j):
                qg, m, kb = stream[j][0:3]
                sb_ = PS_S[(sbase + j) % len(PS_S)]
                if kind == "B" and m == 1:
                    kl, r_kl = QT[:, k2slot, kb * 128:(kb + 1) * 128], r_QT[k2slot]
                else:
                    kl, r_kl = KT[:, kslot, kb * 128:(kb + 1) * 128], r_KT[kslot]
                PE.op(lambda e: e.matmul(psb[sb_][:, 0:QG], lhsT=kl,
                                         rhs=QT[:, qslot, qg * QG:(qg + 1) * QG], start=True, stop=True),
                      [r_kl, r_QT[qslot]], [r_ps[sb_]])

            def emit_exp(j):
                qg, m, kb = stream[j][0:3]
                sb_ = PS_S[(sbase + j) % len(PS_S)]
                pi = pstate["n"] % 6
                pstate["n"] += 1
                if kind == "A":
                    ACT.op(lambda e: e.activation(out=pT[:, pi, 0:QG], in_=psb[sb_][:, 0:QG], func=AF.Exp, scale=scale),
                           [r_ps[sb_]], [r_pT[pi]])
                else:
                    ei = pi % 3
                    ACT.op(lambda e: e.activation(out=a_ebuf[ei][:, 0:QG], in_=psb[sb_][:, 0:QG], func=AF.Exp, scale=scale),
                           [r_ps[sb_]], [r_ebuf[ei]])
                    j0 = c.J0 - kb * 128 + qg * QG
                    DVE.op(lambda e: e.tensor_tensor(out=pT[:, pi, 0:QG], in0=a_ebuf[ei][:, 0:QG],
                                                     in1=a_strip[strip_i][:, j0:j0 + QG], op=ALU.mult),
                           [r_ebuf[ei], r_strip[strip_i]], [r_pT[pi]])
                return pi

            def emit_pv(j, pi):
                qg, m, kb, act_qs, last_kb, gstart, gend = stream[j]
                if gstart:
                    started.clear()
                for n_, qs in enumerate(act_qs):
                    bank = oset[qs // 2]
                    first_in_bank = bank not in started
                    started.add(bank)
                    c0 = (qs % 2) * 129
                    PE.op(lambda e, qs=qs, bank=bank, c0=c0, first_in_bank=first_in_bank: e.matmul(
                        psb[bank][:, c0:c0 + 129], lhsT=pT[:, pi, qs * 128:(qs + 1) * 128],
                        rhs=V[:, kb, vslot, 0:129], start=first_in_bank, stop=(kb == last_kb[qs]),
                        skip_group_check=True),
                        [r_pT[pi], r_V[vslot]], [r_ps[bank]], inc=(n_ == len(act_qs) - 1))

            def group_end(j):
                qg, m = stream[j][0:2]
                drain_pending()
                if kind == "B" and m == 0:
                    for pr in range(QB // 2):
                        b1 = oset[pr]
                        o1v = psb[b1][:, 0:258].rearrange("p (q d) -> p q d", q=2)
                        rd = stc[f"rd{2 * pr}"]
                        r_rd = r_st[f"rd{2 * pr}"]
                        DVE.op(lambda e: e.reciprocal(out=rd.rearrange("p (q o) -> p q o", o=1), in_=o1v[:, :, 128:129]),
                               [r_ps[b1]], [r_rd])
                        rdb = rd.rearrange("p (q o) -> p q o", o=1).broadcast_to([128, 2, 128])
                        DVE.op(lambda e: e.tensor_tensor(out=a_o1[pr][:], in0=o1v[:, :, 0:128], in1=rdb, op=ALU.mult),
                               [r_ps[b1], r_rd], [r_o1[pr]])
                    return
                for pr in range(QB // 2):
                    fi = fstate["n"] % 2
                    fstate["n"] += 1
                    if kind != "B":
                        b1 = osets[0][pr]
                        o1v = psb[b1][:, 0:258].rearrange("p (q d) -> p q d", q=2)
                        rd = stc[f"rd{2 * fi}"]
                        r_rd = r_st[f"rd{2 * fi}"]
                        DVE.op(lambda e: e.reciprocal(out=rd.rearrange("p (q o) -> p q o", o=1), in_=o1v[:, :, 128:129]),
                               [r_ps[b1]], [r_rd])
                        DVE.op(lambda e: e.tensor_scalar(out=rd, in0=rd, scalar1=0.5, scalar2=None, op0=ALU.mult),
                               [r_rd], [r_rd])
                        rdb = rd.rearrange("p (q o) -> p q o", o=1).broadcast_to([128, 2, 128])
                        DVE.op(lambda e: e.tensor_tensor(out=a_yt[fi][:], in0=o1v[:, :, 0:128], in1=rdb, op=ALU.mult),
                               [r_ps[b1], r_rd], [r_yt[fi]])
                    else:
                        b2 = osets[1][pr]
                        o2v = psb[b2][:, 0:258].rearrange("p (q d) -> p q d", q=2)
                        rd2 = stc[f"rd{2 * pr + 1}"]
                        r_rd2 = r_st[f"rd{2 * pr + 1}"]
                        nl = stc[f"nl{fi}"]
                        r_nl = r_st[f"nl{fi}"]
                        DVE.op(lambda e: e.reciprocal(out=rd2.rearrange("p (q o) -> p q o", o=1), in_=o2v[:, :, 128:129]),
                               [r_ps[b2]], [r_rd2])
                        DVE.op(lambda e: e.tensor_scalar(out=nl, in0=rd2, scalar1=stc["nlam"], scalar2=None, op0=ALU.mult),
                               [r_rd2, r_st["nlam"]], [r_nl])
                        for q in range(2):
                            DVE.op(lambda e, q=q: e.scalar_tensor_tensor(out=a_dd[fi][:, q, :], in0=o2v[:, q, 0:128],
                                                                         scalar=nl[:, q:q + 1], in1=a_o1[pr][:, q, :],
                                                                         op0=ALU.mult, op1=ALU.add),
                                   [r_ps[b2], r_nl, r_o1[pr]], [r_dd[fi]])
                    is_last = (qg == NQG - 1 and pr == QB // 2 - 1)
                    pending.append(fin_gen(kind, fi, pr, qg, gslot, yslot, head_feat, is_last))
                if kind not in DEFER_KINDS:
                    drain_pending()

            pis = {}
            for j in range(min(LOOK, NS)):
                emit_qk(j)
                pis[j] = emit_exp(j)
            for j in range(NS):
                if j + LOOK < NS:
                    emit_qk(j + LOOK)
                    pis[j + LOOK] = emit_exp(j + LOOK)
                emit_pv(j, pis[j])
                gend = stream[j][6]
                do_fill(extra=((EXTRA_V if fill["i"] <= NT else EXTRA_Q) if gend else 0))
                if gend:
                    group_end(j)
            advance(tot_steps + len(filler))
            fbstate["on"] = False
            fbstate["held"] = None
            mpool["banks"] = list(range(8))

        pending = []
        import os as _os
        DEFER_KINDS = _os.environ.get("DEFER_KINDS", "ABC")
        EXTRA_V = int(_os.environ.get("EXTRA_V", "28"))
        EXTRA_Q = int(_os.environ.get("EXTRA_Q", "9"))

        def step_pending():
            for g_ in list(pending):
                try:
                    next(g_)
                except StopIteration:
                    pending.remove(g_)

        def drain_pending():
            while pending:
                step_pending()

        def fin_gen(kind, fi, pr, qg, gslot, yslot, head_feat, is_last):
            if kind == "B":
                ACT.op(lambda e: e.activation(out=a_sqd[:], in_=a_dd[fi][:], func=AF.Square, scale=128.0 ** -0.5),
                       [r_dd[fi]], [r_sqd])
                dms, drs = stc[f"dms{fi}"], stc[f"drs{fi}"]
                DVE.op(lambda e: e.tensor_reduce(out=dms, in_=a_sqd[:], axis=AX.X, op=ALU.add), [r_sqd],
                       [r_st[f"dms{fi}"]])
                yield
                POOL.op(lambda e: e.tensor_scalar(out=stc["dln"], in0=dms, scalar1=SUBLN_EPS, scalar2=None, op0=ALU.add),
                        [r_st[f"dms{fi}"]], [r_st["dln"]])
                POOL.op(lambda e: e.tensor_tensor(out=drs, in0=stc["dln"], in1=nhalf[:, 0:2], op=ALU.pow),
                        [r_st["dln"], r_nhalf], [r_st[f"drs{fi}"]])
                yield
                yield
                yield
                for q in range(2):
                    DVE.op(lambda e, q=q: e.scalar_tensor_tensor(out=a_yt[fi][:, q, :], in0=a_dd[fi][:, q, :],
                                                                 scalar=drs[:, q:q + 1], in1=gsub_b[:],
                                                                 op0=ALU.mult, op1=ALU.mult),
                           [r_dd[fi], r_st[f"drs{fi}"], r_gsub], [r_yt[fi]])
                yield
            else:
                yield
            bt = PS_M[1] if fbstate["held"] == PS_M[0] else PS_M[0]
            pv = psb[bt][:].bitcast(BF16)
            for q in range(2):
                PE.op(lambda e, q=q: e.transpose(out=pv[:, q * 128:(q + 1) * 128], in_=a_yt[fi][:, q, :],
                                                 identity=ident[:]),
                      [r_yt[fi], r_ident], [r_ps[bt]], inc=(q == 1))
            t0 = qg * QG + pr * 256
            DVE.op(lambda e: e.tensor_tensor(out=ybuf[:, yslot, t0:t0 + 256], in0=pv[:, 0:256],
                                             in1=sgT[:, gslot, t0:t0 + 256], op=ALU.mult),
                   [r_ps[bt], r_sg[gslot]], [r_yb[yslot]])
            if is_last:
                POOL.dma(k, out=yT_d[head_feat], in_=ybuf[:, yslot, :], reads=[r_yb[yslot]], writes=[r_yTd[head_feat]],
                         sres=r_yb[yslot])

        pstate = {"n": 0}
        fstate = {"n": 0}
        sstate = {"n": 0}

        def load_attn_consts(need_cmul):
            SP.dma(k, out=a_dabs, in_=c_dabs, reads=[r_in], writes=[r_dabs], sres=r_dabs)
            if need_cmul:
                half = c.W // 2
                for hh in range(2):
                    POOL.dma(k, out=a_cmul[:, hh * half:(hh + 1) * half], in_=c_cmul[:, hh * half:(hh + 1) * half],
                             reads=[r_in], writes=[r_cmul], sres=r_cmul, chain=True)

        def make_strip(si, slope, with_c):
            ACT.op(lambda e: e.activation(out=a_strip[si], in_=a_dabs, func=AF.Exp, scale=-float(slope)),
                   [r_dabs], [r_strip[si]])
            if with_c:
                DVE.op(lambda e: e.tensor_tensor(out=a_strip[si], in0=a_strip[si], in1=a_cmul, op=ALU.mult),
                       [r_strip[si], r_cmul], [r_strip[si]])

        def oproj_phase(wb, wres, hsrc, hsrc_res, hdst, hdst_res, row0):
            for kc in range(KC):
                SP.dma(k, out=U[:, kc, :], in_=yT_d[kc], reads=[r_yTd[kc]], writes=[r_yT[kc]], sres=r_yT[kc])
            wslot = load_w(wb, 0, wres)
            n = 0
            for cg in range(c.NOC):
                nxt = load_w(wb, cg + 1, wres) if cg + 1 < c.NOC else None
                for tt in range(NT):
                    i = n % 2
                    n += 1
                    rows = slice(row0 + tt * 128, row0 + (tt + 1) * 128)
                    SP.dma(k, out=o_hres[i], in_=hsrc[rows, cg * 512:(cg + 1) * 512], reads=[hsrc_res[tt]],
                           writes=[r_hres[i]], sres=r_hres[i])
                    b = mbank()
                    for kc in range(KC):
                        PE.op(lambda e, kc=kc, b=b, tt=tt, wslot=wslot: e.matmul(
                            psb[b][:], lhsT=U[:, kc, tt * 128:(tt + 1) * 128], rhs=wsl[wslot][:, kc, :],
                            start=(kc == 0), stop=(kc == KC - 1)),
                            [r_wsl[wslot], r_yT[kc]], [r_ps[b]], inc=(kc == KC - 1))
                    DVE.op(lambda e, i=i, b=b: e.tensor_tensor(out=o_hout[i], in0=psb[b][:], in1=o_hres[i], op=ALU.add),
                           [r_ps[b], r_hres[i]], [r_hout[i]])
                    POOL.dma(k, out=hdst[rows, cg * 512:(cg + 1) * 512], in_=o_hout[i], reads=[r_hout[i]],
                             writes=[hdst_res[tt]], sres=r_hout[i])
                wslot = nxt

        slopes_b = [2.0 ** (-8.0 * (i + 1) / c.BH) for i in range(c.BH)]
        slopes_c = [2.0 ** (-8.0 * (i + 1) / c.CH) for i in range(c.CH)]
        x_res = [r_in] * NT
        for s in range(NSEQ):
            row0 = s * S
            mark("norm0")
            norm_phase(x, x_res, ln_even_g, row0=row0)
            load_attn_consts(False)
            ti = 0
            ws = load_w(wb_ev, ti, r_wev[ti], ncols=c.AKV * 256); ti += 1
            nxt = load_w(wb_ev, ti, r_wev[ti]); ti += 1
            mark("A_kv")
            a_kv_phase(ws)
            yslot = 0

            def b_slots(h):
                kv = (c.AKV + h) % 2
                return kv, kv, kv, h % 2

            def b_items(wslot, h):
                qs, ks, vs, gs = b_slots(h)
                k2 = 2 + h % 2
                its = proj_v_items(wslot, 384, vs)
                its += proj_fm_items(wslot, 0, QT[:, qs, :], r_QT[qs])

                def kevac(b, tg):
                    DVE.op(lambda e: e.tensor_scalar(out=KT[:, ks, tg * QG:(tg + 1) * QG], in0=psb[b][:, 0:QG],
                                                     scalar1=mk[:, 0:1], scalar2=None, op0=ALU.mult),
                           [r_ps[b], r_mk], [r_KT[ks]])
                    DVE.op(lambda e: e.tensor_scalar(out=QT[:, k2, tg * QG:(tg + 1) * QG], in0=psb[b][:, 0:QG],
                                                     scalar1=mk[:, 1:2], scalar2=None, op0=ALU.mult),
                           [r_ps[b], r_mk], [r_QT[k2]])
                its += proj_fm_items(wslot, 128, None, None, evac=kevac)
                its += proj_fm_items(wslot, 256, sgT[:, gs, :], r_sg[gs], act_silu=True)
                return its

            for g in range(c.AKV):
                wq = nxt
                nxt = load_w(wb_ev, ti, r_wev[ti]); ti += 1
                mark("A_q")
                a_q_phase(wq)
                wg = nxt
                nxt = load_w(wb_ev, ti, r_wev[ti]); ti += 1
                mark("A_gproj")
                run_items(proj_fm_items(wg, 0, sgT[:, 0, :], r_sg[0], act_silu=True))
                for j in range(c.AG):
                    if j + 1 < c.AG:
                        fl = proj_fm_items(wg, (j + 1) * 128, sgT[:, (j + 1) % 2, :], r_sg[(j + 1) % 2], act_silu=True)
                    elif g == c.AKV - 1:
                        wB = nxt
                        if ti < len(ev_tiles):
                            nxt = load_w(wb_ev, ti, r_wev[ti]); ti += 1
                        fl = b_items(wB, 0)
                    else:
                        fl = ()
                    mark("A_attn")
                    attention(j, g, g, j % 2, 0, g * c.AG + j, "A", 128.0 ** -0.5, filler=fl)
                    yslot += 1
            for h in range(c.BH):
                qs, ks, vs, gs = b_slots(h)
                make_strip(h % 2, slopes_b[h], False)
                if h + 1 < c.BH:
                    wB = nxt
                    if ti < len(ev_tiles):
                        nxt = load_w(wb_ev, ti, r_wev[ti]); ti += 1
                    fl = b_items(wB, h + 1)
                else:
                    fl = ()
                mark("B_attn")
                attention(qs, ks, vs, gs, 0, c.AH + h, "B", 64.0 ** -0.5, strip_i=h % 2, nmaps=2, filler=fl,
                          k2slot=2 + h % 2)
                yslot += 1
            drain_pending()
            mark("oproj0")
            oproj_phase(wb_oe, r_woe, x, x_res, h1_d, r_h1[s], row0)
            mark("norm1")
            norm_phase(h1_d, r_h1[s], ln_odd_g, row0=row0)
            load_attn_consts(True)
            wC = load_w(wb_od, 0, r_wod[0])
            nxt = load_w(wb_od, 1, r_wod[1])
            mark("C_proj")
            run_items(head_items(wC, 0))
            for h in range(c.CH):
                sl = h % 2
                make_strip(sl, slopes_c[h], True)
                if h + 1 < c.CH:
                    wC = nxt
                    if h + 2 < c.CH:
                        nxt = load_w(wb_od, h + 2, r_wod[h + 2])
                    fl = head_items(wC, (h + 1) % 2)
                else:
                    fl = ()
                mark("C_attn")
                attention(sl, sl, sl, sl, 0, h, "C", 128.0 ** -0.5, strip_i=sl, filler=fl)
                yslot += 1
            drain_pending()
            mark("oproj1")
            oproj_phase(wb_oo, r_woo, h1_d, r_h1[s], h2_d, r_h2[s], row0)
            mark("fnorm")
            norm_phase(h2_d, r_h2[s], final_norm_g, to_out=out, out_res=r_out, row0=row0)
        mark("end")

        for (sem, val) in list(r_out.w.values()):
            POOL.eng.wait_ge(sem, val)

        @block.gpsimd
        def _(e):
            POOL.eng.replay(e)

        @block.sync
        def _(e):
            SP.eng.replay(e)

        @block.scalar
        def _(e):
            ACT.eng.replay(e)

        @block.vector
        def _(e):
            DVE.eng.replay(e)

        @block.tensor
        def _(e):
            PE.eng.replay(e)

        build.stats = {e.key: (e.nins, e.nwaits) for e in (PE, ACT, DVE, POOL, SP)}
        build.nsem = k.nsem
        build.marks = marks
    return nc


def host_consts(cfg):
    S, W, J0 = cfg.S, cfg.W, cfg.J0
    GRID_W, PAIRS = 64, 32
    t = np.arange(S)
    row = (t // GRID_W).astype(np.float32)
    col = (t % GRID_W).astype(np.float32)
    inv = (np.float32(10000.0) ** (-np.arange(PAIRS, dtype=np.float32) / np.float32(PAIRS))).astype(np.float32)
    ar = (row[:, None] * inv[None, :]).astype(np.float32)
    ac = (col[:, None] * inv[None, :]).astype(np.float32)
    cr, sr, cc, sc = np.cos(ar), np.sin(ar), np.cos(ac), np.sin(ac)
    ropeC = np.concatenate([cr, cr, cc, cc], axis=1).astype(np.float32)
    ropeS = np.concatenate([-sr, sr, -sc, sc], axis=1).astype(np.float32)
    p = np.arange(128)[:, None]
    j = np.arange(W)[None, :]
    d = np.abs(j - p - J0)
    dabs = d.astype(np.int16)
    cm = (d <= 64).astype(np.float32) + ((d % 4 == 0) & (d <= 256)).astype(np.float32) \
        + ((d % 16 == 0) & (d <= 1024)).astype(np.float32)
    return {"c_ident": np.eye(128, dtype=np.float32), "c_ropeC": ropeC, "c_ropeS": ropeS,
            "c_dabs": dabs, "c_cmul": cm.astype(np.float32)}


def run(cfg, inputs, trace=False):
    nc = build(cfg)
    consts = host_consts(cfg)
    x = np.ascontiguousarray(np.asarray(inputs["x"], dtype=np.float32))
    B = x.shape[0]
    assert B == N_CORES * cfg.NSEQ
    xs = x.reshape(N_CORES, cfg.NSEQ * cfg.S, cfg.D)
    shared = {}
    for n in ("ln_even_g", "w_in_even", "a_q_norm_g", "a_k_norm_g", "b_lambda_q1", "b_lambda_k1", "b_lambda_q2",
              "b_lambda_k2", "b_subln_g", "w_out_even", "ln_odd_g", "w_in_odd", "w_out_odd"):
        a = np.asarray(inputs[n], dtype=np.float32)
        shared[n] = np.ascontiguousarray(a[0])
    shared["final_norm_g"] = np.ascontiguousarray(np.asarray(inputs["final_norm_g"], dtype=np.float32))
    shared.update(consts)
    in_maps = [dict(shared, x=xs[i]) for i in range(N_CORES)]
    res = run_bass_kernel_spmd(nc, in_maps, core_ids=list(range(N_CORES)), **({"trace": True} if trace else {}))
    outs = [np.asarray(r["out"], dtype=np.float32) for r in res.results]
    full = np.stack(outs, axis=0).reshape(B, cfg.S, cfg.D)
    return full, res


def kernel(**inputs):
    cfg = Cfg()
    full, _ = run(cfg, inputs)
    return full
```

```python
import math
from contextlib import ExitStack

import numpy as np
import concourse.bass as bass
import concourse.mybir as mybir
from concourse.bass_utils import run_bass_kernel_spmd

F32 = mybir.dt.float32
BF16 = mybir.dt.bfloat16
I16 = mybir.dt.int16
AF = mybir.ActivationFunctionType
ALU = mybir.AluOpType
AX = mybir.AxisListType

NORM_EPS = 1e-6
SUBLN_EPS = 1e-5
N_CORES = 8


class Cfg:
    def __init__(self, D=2048, S=2048, NSEQ=2):
        self.D, self.S, self.NSEQ = D, S, NSEQ
        self.KC = D // 128
        self.NT = S // 128
        self.QG = min(512, S)
        self.NQG = S // self.QG
        self.NH = D // 128
        self.AH = self.NH // 2
        self.AKV = max(1, self.AH // 4)
        self.AG = self.AH // self.AKV
        self.BH = self.NH - self.AH
        self.CH = self.NH
        self.oqa = 0
        self.oka = self.AH * 128
        self.ova = self.oka + self.AKV * 128
        self.oqb = self.ova + self.AKV * 128
        self.okb = self.oqb + self.BH * 128
        self.ovb = self.okb + self.BH * 128
        self.ogate = self.ovb + self.BH * 128
        self.IN_EVEN = self.ogate + self.NH * 128
        self.IN_ODD = 4 * D
        self.J0 = S - 128
        self.W = 2 * S - 128
        self.NOC = D // 512


class Region:
    def __init__(self, name):
        self.name = name
        self.cur = None
        self.toks = {}
        self.barrier = {}


class Res:
    def __init__(self, name, region=None, gen=None):
        self.name = name
        self.w = {}
        self.r = {}
        self.region = region
        self.gen = gen
        self.dsem = None
        self.dval = 0


def _upd(d, k, sem, val):
    if k not in d or d[k][1] < val:
        d[k] = (sem, val)


class RecIns:
    def __init__(self, entry):
        self.entry = entry

    def then_inc(self, sem, n):
        self.entry[3] = (sem, n)
        return self


class Rec:
    def __init__(self):
        self.prog = []

    def __getattr__(self, name):
        def f(*a, **kw):
            entry = [name, a, kw, None]
            self.prog.append(entry)
            return RecIns(entry)
        return f

    def replay(self, e):
        for name, a, kw, inc in self.prog:
            ins = getattr(e, name)(*a, **kw)
            if inc is not None:
                ins.then_inc(*inc)


class Eng:
    def __init__(self, key, eng, sem, selfsync):
        self.key, self.real, self.sem, self.selfsync = key, eng, sem, selfsync
        self.eng = Rec()
        self.cnt = 0
        self.waited = {}
        self.nwaits = 0
        self.nins = 0

    def _need(self, reads, writes, extra=()):
        need = {}

        def add(d, allow_self):
            for k, (sem, val) in d.items():
                if k == self.key and not allow_self:
                    continue
                if k not in need or need[k][1] < val:
                    need[k] = (sem, val)

        for r in reads:
            if r.region is not None:
                assert r.region.cur == r.gen, f"read of {r.name} while region holds {r.region.cur}"
            add(r.w, self.selfsync)
        for w in writes:
            if w.region is not None:
                if w.region.cur != w.gen:
                    w.region.barrier = dict(w.region.toks)
                    w.region.cur = w.gen
                add(w.region.barrier, self.selfsync)
            add(w.w, self.selfsync)
            add(w.r, self.selfsync)
        for k, sem, val in extra:
            if k not in need or need[k][1] < val:
                need[k] = (sem, val)
        for k, (sem, val) in need.items():
            if self.waited.get(k, 0) < val:
                self.eng.wait_ge(sem, val)
                self.waited[k] = val
                self.nwaits += 1

    def _mark(self, k, sem, val, reads, writes):
        for r in reads:
            _upd(r.r, k, sem, val)
            if r.region is not None:
                _upd(r.region.toks, k, sem, val)
        for w in writes:
            _upd(w.w, k, sem, val)
            if w.region is not None:
                _upd(w.region.toks, k, sem, val)

    def op(self, fn, reads=(), writes=(), inc=True):
        psr = [r for r in reads if r.name.startswith("ps")]
        if psr:
            reads = [r for r in reads if not r.name.startswith("ps")]
            writes = list(writes) + psr
        self._need(reads, writes)
        ins = fn(self.eng)
        self.nins += 1
        if inc:
            ins.then_inc(self.sem, 1)
            self.cnt += 1
            val = self.cnt
        else:
            val = self.cnt + 1
        self._mark(self.key, self.sem, val, reads, writes)
        return ins

    def dma(self, K, out, in_, reads, writes, sres, chain=False):
        if sres.dsem is None:
            sres.dsem = K.new_sem("d_" + sres.name)
        k = "d_" + sres.name
        extra = ()
        if sres.dval > 0 and not chain:
            extra = ((k, sres.dsem, sres.dval),)
        self._need(reads, writes, extra)
        ins = self.eng.dma_start(out=out, in_=in_)
        ins.then_inc(sres.dsem, 16)
        self.nins += 1
        sres.dval += 16
        self._mark(k, sres.dsem, sres.dval, reads, writes)
        return ins


class K:
    def __init__(self, nc, es):
        self.nc, self.es = nc, es
        self.nsem = 0

    def new_sem(self, name):
        self.nsem += 1
        return self.es.enter_context(self.nc.semaphore(name))

    def sb(self, name, shape, dt):
        return self.es.enter_context(self.nc.sbuf_tensor(name, shape, dt))

    def ps(self, name, shape, dt):
        return self.es.enter_context(self.nc.psum_tensor(name, shape, dt))


def weight_tiles(cfg):
    c = cfg
    ev = []
    ev.append(("Akv", [p for g in range(c.AKV) for p in ((c.oka + g * 128, 128), (c.ova + g * 128, 128))]))
    for g in range(c.AKV):
        ev.append((f"Aq{g}", [(c.oqa + g * c.AG * 128, c.AG * 128)]))
        ev.append((f"Ag{g}", [(c.ogate + g * c.AG * 128, c.AG * 128)]))
    for h in range(c.BH):
        ev.append((f"B{h}", [(c.oqb + h * 128, 128), (c.okb + h * 128, 128),
                             (c.ogate + (c.AH + h) * 128, 128), (c.ovb + h * 128, 128)]))
    od = []
    for h in range(c.CH):
        od.append((f"C{h}", [(h * 128, 128), (c.D + h * 128, 128), (3 * c.D + h * 128, 128), (2 * c.D + h * 128, 128)]))
    out = [(f"O{i}", [(i * 512, 512)]) for i in range(c.NOC)]
    return ev, od, out


def build(cfg):
    c = cfg
    D, S, KC, NT, QG, NQG, NSEQ = c.D, c.S, c.KC, c.NT, c.QG, c.NQG, c.NSEQ
    TOK = NSEQ * S
    nc = bass.Bass("TRN2", target_bir_lowering=False)

    def dram(name, shape, dt, kind):
        return nc.dram_tensor(name, shape, dt, kind=kind).ap()

    x = dram("x", [TOK, D], F32, "ExternalInput")
    ln_even_g = dram("ln_even_g", [D], F32, "ExternalInput")
    w_in_even = dram("w_in_even", [D, c.IN_EVEN], F32, "ExternalInput")
    a_q_norm_g = dram("a_q_norm_g", [128], F32, "ExternalInput")
    a_k_norm_g = dram("a_k_norm_g", [128], F32, "ExternalInput")
    lamv = [dram(n, [64], F32, "ExternalInput") for n in ("b_lambda_q1", "b_lambda_k1", "b_lambda_q2", "b_lambda_k2")]
    b_subln_g = dram("b_subln_g", [128], F32, "ExternalInput")
    w_out_even = dram("w_out_even", [D, D], F32, "ExternalInput")
    ln_odd_g = dram("ln_odd_g", [D], F32, "ExternalInput")
    w_in_odd = dram("w_in_odd", [D, c.IN_ODD], F32, "ExternalInput")
    w_out_odd = dram("w_out_odd", [D, D], F32, "ExternalInput")
    final_norm_g = dram("final_norm_g", [D], F32, "ExternalInput")
    c_ident = dram("c_ident", [128, 128], F32, "ExternalInput")
    c_ropeC = dram("c_ropeC", [S, 128], F32, "ExternalInput")
    c_ropeS = dram("c_ropeS", [S, 128], F32, "ExternalInput")
    c_dabs = dram("c_dabs", [128, c.W], I16, "ExternalInput")
    c_cmul = dram("c_cmul", [128, c.W], F32, "ExternalInput")
    out = dram("out", [TOK, D], F32, "ExternalOutput")

    ev_tiles, od_tiles, out_tiles = weight_tiles(c)
    wb_ev = dram("wb_ev", [len(ev_tiles), 128, KC, 512], BF16, "Internal")
    wb_oe = dram("wb_oe", [c.NOC, 128, KC, 512], BF16, "Internal")
    wb_od = dram("wb_od", [len(od_tiles), 128, KC, 512], BF16, "Internal")
    wb_oo = dram("wb_oo", [c.NOC, 128, KC, 512], BF16, "Internal")
    yT_d = dram("yT_d", [c.NH, 128, S], BF16, "Internal")
    h1_d = dram("h1_d", [TOK, D], F32, "Internal")
    h2_d = dram("h2_d", [TOK, D], F32, "Internal")

    es = ExitStack()
    with es:
        k = K(nc, es)
        U = k.sb("U", [128, KC, S], BF16)
        wsl = [k.sb(f"wsl{i}", [128, KC, 512], BF16) for i in range(2)]
        QT = k.sb("QT", [128, 4, S], BF16)
        KT = k.sb("KT", [128, 2, S], BF16)
        V = k.sb("V", [128, NT, 2, 130], BF16)
        sgT = k.sb("sgT", [128, 2, S], BF16)
        ybuf = k.sb("ybuf", [128, 1, S], BF16)
        pT = k.sb("pT", [128, 6, 512], BF16)
        ident = k.sb("ident", [128, 128], BF16)
        identf = k.sb("identf", [128, 128], F32)
        gq_b = k.sb("gq_b", [128, 128], F32)
        gk_b = k.sb("gk_b", [128, 128], F32)
        gsub_b = k.sb("gsub_b", [128, 128], F32)
        lamt = k.sb("lamt", [128, 4, 64], F32)
        lamj = k.sb("lamj", [128, 64], F32)
        st = k.sb("st", [128, 64], F32)
        XB = 53 * 1024 // 2
        X = k.sb("X", [128, XB], BF16)

        xoff = {}

        def xcarve(gen, name, nbytes, dt, shape_pat=None, **kw):
            off = xoff.get(gen, 0)
            assert off % 4 == 0
            xoff[gen] = off + ((nbytes + 3) // 4) * 4
            assert xoff[gen] <= XB * 2, (gen, name, xoff[gen])
            ap = X[:, off // 2:(off + nbytes) // 2]
            if dt != BF16:
                ap = ap.bitcast(dt)
            if shape_pat:
                ap = ap.rearrange(shape_pat, **kw)
            return ap

        regX = Region("X")
        regU = Region("U")

        n_hb = [xcarve("norm", f"hb{i}", D * 4, F32) for i in range(2)]
        n_ub = [xcarve("norm", f"ub{i}", D * 2, BF16) for i in range(2)]
        n_gb = xcarve("norm", "gb", D * 4, F32)
        n_junk = xcarve("norm", "junk", D * 2, BF16)
        n_ob = [xcarve("norm", f"ob{i}", D * 4, F32) for i in range(2)]
        r_hb = [Res(f"n_hb{i}", regX, "norm") for i in range(2)]
        r_ub = [Res(f"n_ub{i}", regX, "norm") for i in range(2)]
        r_gb = Res("n_gb", regX, "norm")
        r_junk = Res("n_junk", regX, "norm")
        r_ob = [Res(f"n_ob{i}", regX, "norm") for i in range(2)]
        a_dabs = xcarve("attn", "dabs", c.W * 2, I16)
        a_cmul = xcarve("attn", "cmul", c.W * 2, BF16)
        a_strip = [xcarve("attn", f"strip{i}", c.W * 2, BF16) for i in range(2)]
        a_ebuf = [xcarve("attn", f"ebuf{i}", 512 * 2, BF16) for i in range(3)]
        a_sqj = xcarve("attn", "sqj", 512 * 2, BF16, "p (h d) -> p h d", d=128)
        a_qn = [xcarve("attn", f"qn{i}", 512 * 4, F32, "p (h d) -> p h d", d=128) for i in range(2)]
        a_tb = [xcarve("attn", f"tb{i}", 512 * 4, F32, "p (h d) -> p h d", d=128) for i in range(2)]
        a_rot = [xcarve("attn", f"rot{i}", 512 * 2, BF16, "p (h d) -> p h d", d=128) for i in range(2)]
        a_rc = [xcarve("attn", f"rc{i}", 128 * 4, F32) for i in range(2)]
        a_rs = [xcarve("attn", f"rs{i}", 128 * 4, F32) for i in range(2)]
        a_o1 = [xcarve("attn", f"o1{i}", 256 * 4, F32, "p (h d) -> p h d", d=128) for i in range(2)]
        a_dd = [xcarve("attn", f"dd{i}", 256 * 4, F32, "p (h d) -> p h d", d=128) for i in range(2)]
        a_sqd = xcarve("attn", "sqd", 256 * 4, F32, "p (h d) -> p h d", d=128)
        a_yt = [xcarve("attn", f"yt{i}", 256 * 2, BF16, "p (h d) -> p h d", d=128) for i in range(2)]
        r_dabs = Res("a_dabs", regX, "attn")
        r_cmul = Res("a_cmul", regX, "attn")
        r_strip = [Res(f"a_strip{i}", regX, "attn") for i in range(2)]
        r_ebuf = [Res(f"a_ebuf{i}", regX, "attn") for i in range(3)]
        r_sqj = [Res(f"a_sqj{i}", regX, "attn") for i in range(4)]
        r_qn = [Res(f"a_qn{i}", regX, "attn") for i in range(2)]
        r_tb = [Res(f"a_tb{i}", regX, "attn") for i in range(2)]
        r_rot = [Res(f"a_rot{i}", regX, "attn") for i in range(2)]
        r_rc = [Res(f"a_rc{i}", regX, "attn") for i in range(2)]
        r_rs = [Res(f"a_rs{i}", regX, "attn") for i in range(2)]
        r_o1 = [Res(f"a_o1{i}", regX, "attn") for i in range(2)]
        r_dd = [Res(f"a_dd{i}", regX, "attn") for i in range(2)]
        r_sqd = Res("a_sqd", regX, "attn")
        r_yt = [Res(f"a_yt{i}", regX, "attn") for i in range(2)]
        o_hres = [xcarve("oproj", f"hres{i}", 512 * 4, F32) for i in range(2)]
        o_hout = [xcarve("oproj", f"hout{i}", 512 * 4, F32) for i in range(2)]
        r_hres = [Res(f"o_hres{i}", regX, "oproj") for i in range(2)]
        r_hout = [Res(f"o_hout{i}", regX, "oproj") for i in range(2)]

        r_uT = [Res(f"uT{t}", regU, "uT") for t in range(NT)]
        r_yT = [Res(f"yT{kc}", regU, "yT") for kc in range(KC)]

        r_wsl = [Res(f"wsl{i}") for i in range(2)]
        r_QT = [Res(f"QT{i}") for i in range(4)]
        r_KT = [Res(f"KT{i}") for i in range(2)]
        r_V = [Res(f"V{i}") for i in range(2)]
        r_sg = [Res(f"sg{i}") for i in range(2)]
        r_yb = [Res(f"yb{i}") for i in range(1)]
        r_pT = [Res(f"pT{i}") for i in range(6)]
        r_ident = Res("ident")
        r_identf = Res("identf")
        r_gq, r_gk, r_gsub, r_lamt, r_lamj = Res("gq"), Res("gk"), Res("gsub"), Res("lamt"), Res("lamj")
        st_names = ["ms0", "ms1", "ln0", "ln1", "rs0", "rs1", "qms0", "qln0", "qrs0", "qms1", "qln1", "qrs1", "rd0", "rd1", "rd2", "rd3",
                    "nl0", "nl1", "dms0", "dms1", "dln", "drs0", "drs1", "ls1", "ls2", "le1", "le2", "lam", "nlam"]
        st_w = {"qms0": 4, "qln0": 4, "qrs0": 4, "qms1": 4, "qln1": 4, "qrs1": 4, "rd0": 2, "rd1": 2, "rd2": 2, "rd3": 2, "nl0": 2, "nl1": 2,
                "dms0": 2, "dms1": 2, "dln": 2, "drs0": 2, "drs1": 2}
        stc, r_st = {}, {}
        _o = 0
        for n in st_names:
            wd = st_w.get(n, 1)
            stc[n] = st[:, _o:_o + wd]
            r_st[n] = Res("st_" + n)
            _o += wd
        assert _o <= 64

        psb = [k.ps(f"ps{i}", [128, 512], F32) for i in range(8)]
        r_ps = [Res(f"ps{i}") for i in range(8)]
        PS_S = [0, 1, 2, 3]
        PS_O = [[4, 5]]
        PS_M = [6, 7]
        LOOK = 3

        r_wev = [Res(f"wev{i}") for i in range(len(ev_tiles))]
        r_wod = [Res(f"wod{i}") for i in range(len(od_tiles))]
        r_woe = Res("woe")
        r_woo = Res("woo")
        r_yTd = [Res(f"yTd{h}") for h in range(c.NH)]
        r_h1 = [[Res(f"h1_{s}_{t}") for t in range(NT)] for s in range(NSEQ)]
        r_h2 = [[Res(f"h2_{s}_{t}") for t in range(NT)] for s in range(NSEQ)]
        r_out = Res("outd")
        r_in = Res("inputs")

        block = es.enter_context(nc.Block())
        PE = Eng("pe", nc.tensor, k.new_sem("s_pe"), False)
        ACT = Eng("act", nc.scalar, k.new_sem("s_act"), True)
        DVE = Eng("dve", nc.vector, k.new_sem("s_dve"), True)
        POOL = Eng("pool", nc.gpsimd, k.new_sem("s_pool"), True)
        SP = Eng("sp", nc.sync, k.new_sem("s_sp"), False)

        marks = []

        def mark(name):
            marks.append((name, PE.nins))

        cast_q = []

        def cast_tiles(src, dst, tiles, ress):
            for i, (nm, pieces) in enumerate(tiles):
                res = ress[i] if isinstance(ress, list) else ress
                cast_q.append((src, dst, i, pieces, res))

        def pump_casts(n):
            for _ in range(n):
                if not cast_q:
                    return
                src, dst, i, pieces, res = cast_q.pop(0)
                o = 0
                for (c0, wd) in pieces:
                    POOL.dma(k, out=dst[i, :, :, o:o + wd],
                             in_=src[:, c0:c0 + wd].rearrange("(kc p) c -> p kc c", p=128),
                             reads=[r_in], writes=[res], sres=res, chain=True)
                    o += wd

        wstate = {"n": 0}

        def load_w(dst_tiles, idx, res, ncols=512):
            while any((q[1] is dst_tiles and q[2] == idx) for q in cast_q):
                pump_casts(1)
            slot = wstate["n"] % 2
            wstate["n"] += 1
            SP.dma(k, out=wsl[slot][:, :, 0:ncols], in_=dst_tiles[idx, :, :, 0:ncols], reads=[res],
                   writes=[r_wsl[slot]], sres=r_wsl[slot])
            return slot

        mstate = {"n": 0}

        mpool = {"banks": list(range(8))}

        def mbank():
            bl = mpool["banks"]
            b = bl[mstate["n"] % len(bl)]
            mstate["n"] += 1
            return b

        evs = {"n": 0}

        def evac_eng():
            evs["n"] += 1
            return ACT if evs["n"] % 2 == 0 else DVE

        def copy_on(E, out_ap, in_ap, reads, writes):
            if E is ACT:
                E.op(lambda e: e.activation(out=out_ap, in_=in_ap, func=AF.Copy), reads, writes)
            else:
                E.op(lambda e: e.tensor_copy(out=out_ap, in_=in_ap), reads, writes)

        def rstd_from_ms(ms_ap, ln_ap, rs_ap, r_ms, r_ln, r_rs, eps):
            ACT.op(lambda e: e.activation(out=ln_ap, in_=ms_ap, func=AF.Ln, bias=eps_ap[eps], scale=1.0),
                   [r_ms, r_eps], [r_ln])
            ACT.op(lambda e: e.activation(out=rs_ap, in_=ln_ap, func=AF.Exp, scale=-0.5), [r_ln], [r_rs])

        epst = k.sb("epst", [128, 2], F32)
        r_eps = Res("eps")
        eps_ap = {NORM_EPS: epst[:, 0:1], SUBLN_EPS: epst[:, 1:2]}
        POOL.op(lambda e: e.memset(epst[:, 0:1], NORM_EPS), [], [r_eps])
        POOL.op(lambda e: e.memset(epst[:, 1:2], SUBLN_EPS), [], [r_eps])
        cast_tiles(w_in_even, wb_ev, ev_tiles, r_wev)
        cast_tiles(w_out_even, wb_oe, out_tiles, r_woe)
        cast_tiles(w_in_odd, wb_od, od_tiles, r_wod)
        cast_tiles(w_out_odd, wb_oo, out_tiles, r_woo)
        pump_casts(3 + 2 * (c.AKV - 1))

        SP.dma(k, out=identf[:], in_=c_ident, reads=[r_in], writes=[r_identf], sres=r_identf)
        DVE.op(lambda e: e.tensor_copy(out=ident[:], in_=identf[:]), [r_identf], [r_ident])
        SP.dma(k, out=gq_b[:], in_=a_q_norm_g.partition_broadcast(128), reads=[r_in], writes=[r_gq], sres=r_gq)
        SP.dma(k, out=gk_b[:], in_=a_k_norm_g.partition_broadcast(128), reads=[r_in], writes=[r_gk], sres=r_gk)
        SP.dma(k, out=gsub_b[:], in_=b_subln_g.partition_broadcast(128), reads=[r_in], writes=[r_gsub], sres=r_gsub)
        lambda_init = 0.8 - 0.6 * math.exp(-0.3 * 0)
        DVE.op(lambda e: e.tensor_scalar(out=gsub_b[:], in0=gsub_b[:], scalar1=0.5 * (1.0 - lambda_init), scalar2=None,
                                         op0=ALU.mult), [r_gsub], [r_gsub])
        for i in range(4):
            SP.dma(k, out=lamt[:, i, :], in_=lamv[i].partition_broadcast(128), reads=[r_in], writes=[r_lamt],
                   sres=r_lamt, chain=True)
        for (a, b, sname, ename) in ((0, 1, "ls1", "le1"), (2, 3, "ls2", "le2")):
            DVE.op(lambda e, a=a, b=b, sname=sname: e.tensor_tensor(out=lamj[:], in0=lamt[:, a, :], in1=lamt[:, b, :],
                                                                     op=ALU.mult), [r_lamt], [r_lamj])
            DVE.op(lambda e, sname=sname: e.tensor_reduce(out=stc[sname], in_=lamj[:], axis=AX.X, op=ALU.add),
                   [r_lamj], [r_st[sname]])
            ACT.op(lambda e, sname=sname, ename=ename: e.activation(out=stc[ename], in_=stc[sname], func=AF.Exp),
                   [r_st[sname]], [r_st[ename]])
        DVE.op(lambda e: e.tensor_tensor(out=stc["lam"], in0=stc["le1"], in1=stc["le2"], op=ALU.subtract),
               [r_st["le1"], r_st["le2"]], [r_st["lam"]])
        DVE.op(lambda e: e.tensor_scalar(out=stc["nlam"], in0=stc["lam"], scalar1=lambda_init, scalar2=-1.0,
                                         op0=ALU.add, op1=ALU.mult), [r_st["lam"]], [r_st["nlam"]])
        tbuf = k.sb("tbuf", [128, 512], F32)
        r_tbuf = Res("tbuf")
        nhalf = k.sb("nhalf", [128, 4], F32)
        r_nhalf = Res("nhalf")
        POOL.op(lambda e: e.memset(nhalf[:], -0.5), [], [r_nhalf])
        mk = k.sb("mk", [128, 2], F32)
        r_mk = Res("mk")
        POOL.op(lambda e: e.memset(mk[0:64, 0:1], 1.0), [], [r_mk])
        POOL.op(lambda e: e.memset(mk[64:128, 0:1], 0.0), [], [r_mk])
        POOL.op(lambda e: e.memset(mk[0:64, 1:2], 0.0), [], [r_mk])
        POOL.op(lambda e: e.memset(mk[64:128, 1:2], 1.0), [], [r_mk])
        POOL.op(lambda e: e.memset(V[:, :, :, 128:130], 1.0), [], [r_V[0], r_V[1]])

        def norm_phase(src, src_res, gvec, to_out=None, out_res=None, row0=0):
            SP.dma(k, out=n_gb, in_=gvec.partition_broadcast(128), reads=[r_in], writes=[r_gb], sres=r_gb)
            def stage_a(tt):
                i = tt % 2
                rows = slice(row0 + tt * 128, row0 + (tt + 1) * 128)
                SP.dma(k, out=n_hb[i], in_=src[rows, :], reads=[src_res[tt]], writes=[r_hb[i]], sres=r_hb[i])
                ms, ln, rs = stc[f"ms{i}"], stc[f"ln{i}"], stc[f"rs{i}"]
                ACT.op(lambda e: e.activation(out=n_junk, in_=n_hb[i], func=AF.Square,
                                              scale=float(D) ** -0.5, accum_out=ms),
                       [r_hb[i]], [r_junk, r_st[f"ms{i}"]])
                rstd_from_ms(ms, ln, rs, r_st[f"ms{i}"], r_st[f"ln{i}"], r_st[f"rs{i}"], NORM_EPS)

            def stage_a2(tt):
                i = tt % 2
                rs = stc[f"rs{i}"]
                dst, r_dst = (n_ub[i], r_ub[i]) if to_out is None else (n_ob[i], r_ob[i])
                DVE.op(lambda e: e.scalar_tensor_tensor(out=dst, in0=n_hb[i], scalar=rs, in1=n_gb,
                                                        op0=ALU.mult, op1=ALU.mult),
                       [r_hb[i], r_st[f"rs{i}"], r_gb], [r_dst])

            def stage_b(tt):
                i = tt % 2
                rows = slice(row0 + tt * 128, row0 + (tt + 1) * 128)
                if to_out is None:
                    for k4 in range(KC // 4):
                        b = mbank()
                        pv = psb[b][:].bitcast(BF16)
                        for j in range(4):
                            kc = k4 * 4 + j
                            PE.op(lambda e, pv=pv, j=j, kc=kc, i=i: e.transpose(
                                out=pv[:, j * 128:(j + 1) * 128], in_=n_ub[i][:, kc * 128:(kc + 1) * 128], identity=ident[:]),
                                [r_ub[i], r_ident], [r_ps[b]], inc=(j == 3))
                        E = evac_eng()
                        copy_on(E, U[:, k4 * 4:(k4 + 1) * 4, tt * 128:(tt + 1) * 128],
                                pv[:, 0:512].rearrange("p (a t) -> p a t", a=4), [r_ps[b]], [r_uT[tt]])
                else:
                    POOL.dma(k, out=to_out[rows, :], in_=n_ob[i], reads=[r_ob[i]], writes=[out_res], sres=r_ob[i])

            stage_a(0)
            stage_a2(0)
            for tt in range(NT):
                if tt + 1 < NT:
                    stage_a(tt + 1)
                stage_b(tt)
                if tt + 1 < NT:
                    stage_a2(tt + 1)

        fbstate = {"on": False, "n": 0, "held": None}

        def fbank():
            if not fbstate["on"]:
                return mbank()
            b = PS_M[fbstate["n"] % 2]
            fbstate["n"] += 1
            fbstate["held"] = b
            return b

        def frelease():
            fbstate["held"] = None

        def proj_fm_items(wslot, col0, dst3, dst_res, act_silu=False, evac=None):
            items = []
            for tg in range(NQG):
                def it(tg=tg):
                    if act_silu:
                        drain_pending()
                    b = fbank()
                    for kc in range(KC):
                        PE.op(lambda e, kc=kc: e.matmul(psb[b][:, 0:QG], lhsT=wsl[wslot][:, kc, col0:col0 + 128],
                                                        rhs=U[:, kc, tg * QG:(tg + 1) * QG],
                                                        start=(kc == 0), stop=(kc == KC - 1)),
                              [r_wsl[wslot]] + r_uT[tg * (QG // 128):(tg + 1) * (QG // 128)], [r_ps[b]], inc=(kc == KC - 1))
                        if kc < KC - 1:
                            yield
                    if evac is not None:
                        evac(b, tg)
                    else:
                        o = dst3[:, tg * QG:(tg + 1) * QG]
                        if act_silu:
                            ACT.op(lambda e: e.activation(out=tbuf[:, 0:QG], in_=psb[b][:, 0:QG], func=AF.Tanh, scale=0.5),
                                   [r_ps[b]], [r_tbuf])
                            DVE.op(lambda e: e.scalar_tensor_tensor(out=o, in0=tbuf[:, 0:QG], scalar=1.0, in1=psb[b][:, 0:QG],
                                                                    op0=ALU.add, op1=ALU.mult),
                                   [r_tbuf, r_ps[b]], [dst_res])
                        else:
                            copy_on(DVE, o, psb[b][:, 0:QG], [r_ps[b]], [dst_res])
                    frelease()
                    yield
                items.append(it)
            return items

        def proj_v_items(wslot, col0, vslot):
            items = []
            for tt in range(NT):
                def it(tt=tt):
                    b = fbank()
                    for kc in range(KC):
                        PE.op(lambda e, kc=kc: e.matmul(psb[b][:, 0:128], lhsT=U[:, kc, tt * 128:(tt + 1) * 128],
                                                        rhs=wsl[wslot][:, kc, col0:col0 + 128], start=(kc == 0),
                                                        stop=(kc == KC - 1)),
                              [r_wsl[wslot], r_uT[tt]], [r_ps[b]], inc=(kc == KC - 1))
                        if kc < KC - 1:
                            yield
                    copy_on(DVE, V[:, tt, vslot, 0:128], psb[b][:, 0:128], [r_ps[b]], [r_V[vslot]])
                    frelease()
                    yield
                items.append(it)
            return items

        def head_items(wslot, sl):
            its = proj_v_items(wslot, 384, sl)
            its += proj_fm_items(wslot, 0, QT[:, sl, :], r_QT[sl])
            its += proj_fm_items(wslot, 128, KT[:, sl, :], r_KT[sl])
            its += proj_fm_items(wslot, 256, sgT[:, sl, :], r_sg[sl], act_silu=True)
            return its

        def run_items(items):
            for it in items:
                for _ in it():
                    pass

        def load_rope(tt):
            i = tt % 2
            SP.dma(k, out=a_rc[i], in_=c_ropeC[tt * 128:(tt + 1) * 128, :], reads=[r_in], writes=[r_rc[i]], sres=r_rc[i])
            SP.dma(k, out=a_rs[i], in_=c_ropeS[tt * 128:(tt + 1) * 128, :], reads=[r_in], writes=[r_rs[i]], sres=r_rs[i])

        qstate = {"n": 0}

        def qk_post(b, c0, nh, gain, r_gain, tt):
            bi = qstate["n"] % 2
            qstate["n"] += 1
            i = tt % 2
            src = psb[b][:, c0:c0 + nh * 128].rearrange("p (h d) -> p h d", d=128)
            qms, qln, qrs = stc[f"qms{bi}"], stc[f"qln{bi}"], stc[f"qrs{bi}"]
            r_qms, r_qln, r_qrs = r_st[f"qms{bi}"], r_st[f"qln{bi}"], r_st[f"qrs{bi}"]
            for h in range(nh):
                ACT.op(lambda e, h=h: e.activation(out=a_sqj[:, h, :], in_=src[:, h, :], func=AF.Square,
                                                   scale=128.0 ** -0.5, accum_out=qms[:, h:h + 1]),
                       [r_ps[b]], [r_sqj[h], r_qms])
            rstd_from_ms(qms[:, 0:nh], qln[:, 0:nh], qrs[:, 0:nh], r_qms, r_qln, r_qrs, NORM_EPS)
            qn, tb, rot = a_qn[bi], a_tb[bi], a_rot[bi]
            for h in range(nh):
                DVE.op(lambda e, h=h: e.scalar_tensor_tensor(out=qn[:, h, :], in0=src[:, h, :],
                                                             scalar=qrs[:, h:h + 1], in1=gain[:],
                                                             op0=ALU.mult, op1=ALU.mult),
                       [r_ps[b], r_qrs, r_gain], [r_qn[bi]])
            qv = qn[:, 0:nh, :].rearrange("p h (s t d) -> p h s t d", s=2, t=2)
            tv = tb[:, 0:nh, :].rearrange("p h (s t d) -> p h s t d", s=2, t=2)
            sv = a_rs[i].rearrange("p (o s t d) -> p o s t d", o=1, s=2, t=2)
            for t in range(2):
                POOL.op(lambda e, t=t: e.tensor_tensor(out=tv[:, :, :, t, :], in0=qv[:, :, :, 1 - t, :],
                                                       in1=sv[:, :, :, t, :].broadcast_to([128, nh, 2, 32]), op=ALU.mult),
                        [r_qn[bi], r_rs[i]], [r_tb[bi]])
            cb = a_rc[i].rearrange("p (o d) -> p o d", o=1).broadcast_to([128, nh, 128])
            DVE.op(lambda e: e.tensor_tensor(out=qn[:, 0:nh, :], in0=qn[:, 0:nh, :], in1=cb, op=ALU.mult),
                   [r_qn[bi], r_rc[i]], [r_qn[bi]])
            DVE.op(lambda e: e.tensor_tensor(out=rot[:, 0:nh, :], in0=qn[:, 0:nh, :], in1=tb[:, 0:nh, :], op=ALU.add),
                   [r_qn[bi], r_tb[bi]], [r_rot[bi]])
            return bi

        def a_kv_phase(wslot):
            ncol = c.AKV * 256
            pend = []
            for tt in range(NT + 1):
                b = None
                if tt < NT:
                    load_rope(tt)
                    b = mbank()
                    for kc in range(KC):
                        PE.op(lambda e, kc=kc, b=b, tt=tt: e.matmul(psb[b][:, 0:ncol], lhsT=U[:, kc, tt * 128:(tt + 1) * 128],
                                                                     rhs=wsl[wslot][:, kc, 0:ncol], start=(kc == 0),
                                                                     stop=(kc == KC - 1)),
                              [r_wsl[wslot], r_uT[tt]], [r_ps[b]], inc=(kc == KC - 1))
                for (ptt, g, bi) in pend:
                    b2 = mbank()
                    pv = psb[b2][:].bitcast(BF16)
                    PE.op(lambda e, pv=pv, bi=bi: e.transpose(out=pv[:, 0:128], in_=a_rot[bi][:, 0, :], identity=ident[:]),
                          [r_rot[bi], r_ident], [r_ps[b2]])
                    copy_on(evac_eng(), KT[:, g, ptt * 128:(ptt + 1) * 128], pv[:, 0:128], [r_ps[b2]], [r_KT[g]])
                pend = []
                if tt < NT:
                    for g in range(c.AKV):
                        copy_on(evac_eng(), V[:, tt, g, 0:128], psb[b][:, g * 256 + 128:g * 256 + 256], [r_ps[b]], [r_V[g]])
                        bi = qk_post(b, g * 256, 1, gk_b, r_gk, tt)
                        pend.append((tt, g, bi))

        def a_q_phase(wslot):
            nh = c.AG
            pend = None
            for tt in range(NT + 1):
                b = None
                if tt < NT:
                    load_rope(tt)
                    b = mbank()
                    for kc in range(KC):
                        PE.op(lambda e, kc=kc, b=b, tt=tt: e.matmul(psb[b][:, 0:nh * 128], lhsT=U[:, kc, tt * 128:(tt + 1) * 128],
                                                                     rhs=wsl[wslot][:, kc, 0:nh * 128], start=(kc == 0),
                                                                     stop=(kc == KC - 1)),
                              [r_wsl[wslot], r_uT[tt]], [r_ps[b]], inc=(kc == KC - 1))
                if pend is not None:
                    ptt, bi = pend
                    b2 = mbank()
                    pv = psb[b2][:].bitcast(BF16)
                    for h in range(nh):
                        PE.op(lambda e, pv=pv, h=h, bi=bi: e.transpose(out=pv[:, h * 128:(h + 1) * 128], in_=a_rot[bi][:, h, :],
                                                                       identity=ident[:]),
                              [r_rot[bi], r_ident], [r_ps[b2]], inc=(h == nh - 1))
                    copy_on(evac_eng(), QT[:, 0:nh, ptt * 128:(ptt + 1) * 128],
                            pv[:, 0:nh * 128].rearrange("p (h t) -> p h t", h=nh), [r_ps[b2]], r_QT[0:nh])
                    pend = None
                if tt < NT:
                    bi = qk_post(b, 0, nh, gq_b, r_gq, tt)
                    pend = (tt, bi)

        ostate = {"n": 0}

        def attention(qslot, kslot, vslot, gslot, yslot, head_feat, kind, scale, strip_i=None, nmaps=1, filler=(),
                      k2slot=None, pre=None):
            QB = QG // 128
            if pre is not None:
                pending.append(pre)
            pump_casts(2)
            mpool["banks"] = list(PS_M)
            filler = list(filler)
            if kind == "C":
                tot_blocks = sum(1 for qg_ in range(NQG) for kb_ in range(NT)
                                 if any(abs(kb_ - (qg_ * QB + qs_)) <= 8 for qs_ in range(QB)))
            else:
                tot_blocks = NQG * NT * nmaps
            fill = {"done": 0, "blk": 0, "i": 0, "cur": None}
            tot_steps = len(filler) * KC
            fbstate["on"] = True

            def advance(n):
                for _ in range(n):
                    while True:
                        if fill["cur"] is None:
                            if fill["i"] >= len(filler):
                                return
                            fill["cur"] = filler[fill["i"]]()
                            fill["i"] += 1
                        try:
                            next(fill["cur"])
                            fill["done"] += 1
                            break
                        except StopIteration:
                            fill["cur"] = None

            def do_fill(extra=0):
                step_pending()
                fill["blk"] += 1
                target = (tot_steps * fill["blk"]) // tot_blocks + extra
                if target > fill["done"]:
                    advance(target - fill["done"])

            oset = PS_O[0]
            osets = [oset, oset]
            stream = []
            for qg_ in range(NQG):
                for m_ in range(nmaps):
                    blocks = []
                    for kb_ in range(NT):
                        act_qs = [qs for qs in range(QB) if (kind != "C" or abs(kb_ - (qg_ * QB + qs)) <= 8)]
                        if act_qs:
                            blocks.append((kb_, act_qs))
                    for bi_, (kb_, act_qs) in enumerate(blocks):
                        last_kb = {qs: max(b_[0] for b_ in blocks if qs in b_[1]) for qs in act_qs}
                        stream.append((qg_, m_, kb_, act_qs, last_kb, bi_ == 0, bi_ == len(blocks) - 1))
            NS = len(stream)
            sbase = sstate["n"]
            sstate["n"] += NS
            started = set()

            def emit_qk(j):
                qg, m, kb = stream[j][0:3]
                sb_ = PS_S[(sbase + j) % len(PS_S)]
                if kind == "B" and m == 1:
                    kl, r_kl = QT[:, k2slot, kb * 128:(kb + 1) * 128], r_QT[k2slot]
                else:
                    kl, r_kl = KT[:, kslot, kb * 128:(kb + 1) * 128], r_KT[kslot]
                PE.op(lambda e: e.matmul(psb[sb_][:, 0:QG], lhsT=kl,
                                         rhs=QT[:, qslot, qg * QG:(qg + 1) * QG], start=True, stop=True),
                      [r_kl, r_QT[qslot]], [r_ps[sb_]])

            def emit_exp(j):
                qg, m, kb = stream[j][0:3]
                sb_ = PS_S[(sbase + j) % len(PS_S)]
                pi = pstate["n"] % 6
                pstate["n"] += 1
                if kind == "A":
                    ACT.op(lambda e: e.activation(out=pT[:, pi, 0:QG], in_=psb[sb_][:, 0:QG], func=AF.Exp, scale=scale),
                           [r_ps[sb_]], [r_pT[pi]])
                else:
                    ei = pi % 3
                    ACT.op(lambda e: e.activation(out=a_ebuf[ei][:, 0:QG], in_=psb[sb_][:, 0:QG], func=AF.Exp, scale=scale),
                           [r_ps[sb_]], [r_ebuf[ei]])
                    j0 = c.J0 - kb * 128 + qg * QG
                    DVE.op(lambda e: e.tensor_tensor(out=pT[:, pi, 0:QG], in0=a_ebuf[ei][:, 0:QG],
                                                     in1=a_strip[strip_i][:, j0:j0 + QG], op=ALU.mult),
                           [r_ebuf[ei], r_strip[strip_i]], [r_pT[pi]])
                return pi

            def emit_pv(j, pi):
                qg, m, kb, act_qs, last_kb, gstart, gend = stream[j]
                if gstart:
                    started.clear()
                for n_, qs in enumerate(act_qs):
                    bank = oset[qs // 2]
                    first_in_bank = bank not in started
                    started.add(bank)
                    c0 = (qs % 2) * 129
                    PE.op(lambda e, qs=qs, bank=bank, c0=c0, first_in_bank=first_in_bank: e.matmul(
                        psb[bank][:, c0:c0 + 129], lhsT=pT[:, pi, qs * 128:(qs + 1) * 128],
                        rhs=V[:, kb, vslot, 0:129], start=first_in_bank, stop=(kb == last_kb[qs]),
                        skip_group_check=True),
                        [r_pT[pi], r_V[vslot]], [r_ps[bank]], inc=(n_ == len(act_qs) - 1))

            def group_end(j):
                qg, m = stream[j][0:2]
                drain_pending()
                if kind == "B" and m == 0:
                    for pr in range(QB // 2):
                        b1 = oset[pr]
                        o1v = psb[b1][:, 0:258].rearrange("p (q d) -> p q d", q=2)
                        rd = stc[f"rd{2 * pr}"]
                        r_rd = r_st[f"rd{2 * pr}"]
                        DVE.op(lambda e: e.reciprocal(out=rd.rearrange("p (q o) -> p q o", o=1), in_=o1v[:, :, 128:129]),
                               [r_ps[b1]], [r_rd])
                        rdb = rd.rearrange("p (q o) -> p q o", o=1).broadcast_to([128, 2, 128])
                        DVE.op(lambda e: e.tensor_tensor(out=a_o1[pr][:], in0=o1v[:, :, 0:128], in1=rdb, op=ALU.mult),
                               [r_ps[b1], r_rd], [r_o1[pr]])
                    return
                for pr in range(QB // 2):
                    fi = fstate["n"] % 2
                    fstate["n"] += 1
                    if kind != "B":
                        b1 = osets[0][pr]
                        o1v = psb[b1][:, 0:258].rearrange("p (q d) -> p q d", q=2)
                        rd = stc[f"rd{2 * fi}"]
                        r_rd = r_st[f"rd{2 * fi}"]
                        DVE.op(lambda e: e.reciprocal(out=rd.rearrange("p (q o) -> p q o", o=1), in_=o1v[:, :, 128:129]),
                               [r_ps[b1]], [r_rd])
                        DVE.op(lambda e: e.tensor_scalar(out=rd, in0=rd, scalar1=0.5, scalar2=None, op0=ALU.mult),
                               [r_rd], [r_rd])
                        rdb = rd.rearrange("p (q o) -> p q o", o=1).broadcast_to([128, 2, 128])
                        DVE.op(lambda e: e.tensor_tensor(out=a_yt[fi][:], in0=o1v[:, :, 0:128], in1=rdb, op=ALU.mult),
                               [r_ps[b1], r_rd], [r_yt[fi]])
                    else:
                        b2 = osets[1][pr]
                        o2v = psb[b2][:, 0:258].rearrange("p (q d) -> p q d", q=2)
                        rd2 = stc[f"rd{2 * pr + 1}"]
                        r_rd2 = r_st[f"rd{2 * pr + 1}"]
                        nl = stc[f"nl{fi}"]
                        r_nl = r_st[f"nl{fi}"]
                        DVE.op(lambda e: e.reciprocal(out=rd2.rearrange("p (q o) -> p q o", o=1), in_=o2v[:, :, 128:129]),
                               [r_ps[b2]], [r_rd2])
                        DVE.op(lambda e: e.tensor_scalar(out=nl, in0=rd2, scalar1=stc["nlam"], scalar2=None, op0=ALU.mult),
                               [r_rd2, r_st["nlam"]], [r_nl])
                        for q in range(2):
                            DVE.op(lambda e, q=q: e.scalar_tensor_tensor(out=a_dd[fi][:, q, :], in0=o2v[:, q, 0:128],
                                                                         scalar=nl[:, q:q + 1], in1=a_o1[pr][:, q, :],
                                                                         op0=ALU.mult, op1=ALU.add),
                                   [r_ps[b2], r_nl, r_o1[pr]], [r_dd[fi]])
                    is_last = (qg == NQG - 1 and pr == QB // 2 - 1)
                    pending.append(fin_gen(kind, fi, pr, qg, gslot, yslot, head_feat, is_last))
                if kind not in DEFER_KINDS:
                    drain_pending()

            pis = {}
            for j in range(min(LOOK, NS)):
                emit_qk(j)
                pis[j] = emit_exp(j)
            for j in range(NS):
                if j + LOOK < NS:
                    emit_qk(j + LOOK)
                    pis[j + LOOK] = emit_exp(j + LOOK)
                emit_pv(j, pis[j])
                gend = stream[j][6]
                do_fill(extra=((EXTRA_V if fill["i"] <= NT else EXTRA_Q) if gend else 0))
                if gend:
                    group_end(j)
            advance(tot_steps + len(filler))
            fbstate["on"] = False
            fbstate["held"] = None
            mpool["banks"] = list(range(8))

        pending = []
        import os as _os
        DEFER_KINDS = _os.environ.get("DEFER_KINDS", "ABC")
        EXTRA_V = int(_os.environ.get("EXTRA_V", "28"))
        EXTRA_Q = int(_os.environ.get("EXTRA_Q", "9"))

        def step_pending():
            for g_ in list(pending):
                try:
                    next(g_)
                except StopIteration:
                    pending.remove(g_)

        def drain_pending():
            while pending:
                step_pending()

        def fin_gen(kind, fi, pr, qg, gslot, yslot, head_feat, is_last):
            if kind == "B":
                ACT.op(lambda e: e.activation(out=a_sqd[:], in_=a_dd[fi][:], func=AF.Square, scale=128.0 ** -0.5),
                       [r_dd[fi]], [r_sqd])
                dms, drs = stc[f"dms{fi}"], stc[f"drs{fi}"]
                DVE.op(lambda e: e.tensor_reduce(out=dms, in_=a_sqd[:], axis=AX.X, op=ALU.add), [r_sqd],
                       [r_st[f"dms{fi}"]])
                yield
                POOL.op(lambda e: e.tensor_scalar(out=stc["dln"], in0=dms, scalar1=SUBLN_EPS, scalar2=None, op0=ALU.add),
                        [r_st[f"dms{fi}"]], [r_st["dln"]])
                POOL.op(lambda e: e.tensor_tensor(out=drs, in0=stc["dln"], in1=nhalf[:, 0:2], op=ALU.pow),
                        [r_st["dln"], r_nhalf], [r_st[f"drs{fi}"]])
                yield
                yield
                yield
                for q in range(2):
                    DVE.op(lambda e, q=q: e.scalar_tensor_tensor(out=a_yt[fi][:, q, :], in0=a_dd[fi][:, q, :],
                                                                 scalar=drs[:, q:q + 1], in1=gsub_b[:],
                                                                 op0=ALU.mult, op1=ALU.mult),
                           [r_dd[fi], r_st[f"drs{fi}"], r_gsub], [r_yt[fi]])
                yield
            else:
                yield
            bt = PS_M[1] if fbstate["held"] == PS_M[0] else PS_M[0]
            pv = psb[bt][:].bitcast(BF16)
            for q in range(2):
                PE.op(lambda e, q=q: e.transpose(out=pv[:, q * 128:(q + 1) * 128], in_=a_yt[fi][:, q, :],
                                                 identity=ident[:]),
                      [r_yt[fi], r_ident], [r_ps[bt]], inc=(q == 1))
            t0 = qg * QG + pr * 256
            DVE.op(lambda e: e.tensor_tensor(out=ybuf[:, yslot, t0:t0 + 256], in0=pv[:, 0:256],
                                             in1=sgT[:, gslot, t0:t0 + 256], op=ALU.mult),
                   [r_ps[bt], r_sg[gslot]], [r_yb[yslot]])
            if is_last:
                POOL.dma(k, out=yT_d[head_feat], in_=ybuf[:, yslot, :], reads=[r_yb[yslot]], writes=[r_yTd[head_feat]],
                         sres=r_yb[yslot])

        pstate = {"n": 0}
        fstate = {"n": 0}
        sstate = {"n": 0}

        def load_attn_consts(need_cmul):
            SP.dma(k, out=a_dabs, in_=c_dabs, reads=[r_in], writes=[r_dabs], sres=r_dabs)
            if need_cmul:
                half = c.W // 2
                for hh in range(2):
                    POOL.dma(k, out=a_cmul[:, hh * half:(hh + 1) * half], in_=c_cmul[:, hh * half:(hh + 1) * half],
                             reads=[r_in], writes=[r_cmul], sres=r_cmul, chain=True)

        def strip_gen(si, slope, with_c, nchunk=4):
            cw = (c.W + nchunk - 1) // nchunk
            cw += cw % 2
            for ci in range(nchunk):
                lo, hi = ci * cw, min(c.W, (ci + 1) * cw)
                if lo >= hi:
                    break
                ACT.op(lambda e: e.activation(out=a_strip[si][:, lo:hi], in_=a_dabs[:, lo:hi], func=AF.Exp,
                                              scale=-float(slope)), [r_dabs], [r_strip[si]])
                if with_c:
                    DVE.op(lambda e: e.tensor_tensor(out=a_strip[si][:, lo:hi], in0=a_strip[si][:, lo:hi],
                                                     in1=a_cmul[:, lo:hi], op=ALU.mult),
                           [r_strip[si], r_cmul], [r_strip[si]])
                yield

        def make_strip(si, slope, with_c):
            for _ in strip_gen(si, slope, with_c):
                pass

        def oproj_phase(wb, wres, hsrc, hsrc_res, hdst, hdst_res, row0):
            for kc in range(KC):
                SP.dma(k, out=U[:, kc, :], in_=yT_d[kc], reads=[r_yTd[kc]], writes=[r_yT[kc]], sres=r_yT[kc])
            wslot = load_w(wb, 0, wres)
            n = 0
            for cg in range(c.NOC):
                nxt = load_w(wb, cg + 1, wres) if cg + 1 < c.NOC else None
                for tt in range(NT):
                    i = n % 2
                    n += 1
                    rows = slice(row0 + tt * 128, row0 + (tt + 1) * 128)
                    SP.dma(k, out=o_hres[i], in_=hsrc[rows, cg * 512:(cg + 1) * 512], reads=[hsrc_res[tt]],
                           writes=[r_hres[i]], sres=r_hres[i])
                    b = mbank()
                    for kc in range(KC):
                        PE.op(lambda e, kc=kc, b=b, tt=tt, wslot=wslot: e.matmul(
                            psb[b][:], lhsT=U[:, kc, tt * 128:(tt + 1) * 128], rhs=wsl[wslot][:, kc, :],
                            start=(kc == 0), stop=(kc == KC - 1)),
                            [r_wsl[wslot], r_yT[kc]], [r_ps[b]], inc=(kc == KC - 1))
                    DVE.op(lambda e, i=i, b=b: e.tensor_tensor(out=o_hout[i], in0=psb[b][:], in1=o_hres[i], op=ALU.add),
                           [r_ps[b], r_hres[i]], [r_hout[i]])
                    POOL.dma(k, out=hdst[rows, cg * 512:(cg + 1) * 512], in_=o_hout[i], reads=[r_hout[i]],
                             writes=[hdst_res[tt]], sres=r_hout[i])
                wslot = nxt

        slopes_b = [2.0 ** (-8.0 * (i + 1) / c.BH) for i in range(c.BH)]
        slopes_c = [2.0 ** (-8.0 * (i + 1) / c.CH) for i in range(c.CH)]
        x_res = [r_in] * NT
        for s in range(NSEQ):
            row0 = s * S
            mark("norm0")
            norm_phase(x, x_res, ln_even_g, row0=row0)
            load_attn_consts(False)
            ti = 0
            ws = load_w(wb_ev, ti, r_wev[ti], ncols=c.AKV * 256); ti += 1
            nxt = load_w(wb_ev, ti, r_wev[ti]); ti += 1
            mark("A_kv")
            a_kv_phase(ws)
            yslot = 0

            def b_slots(h):
                kv = (c.AKV + h) % 2
                return kv, kv, kv, h % 2

            def b_items(wslot, h):
                qs, ks, vs, gs = b_slots(h)
                k2 = 2 + h % 2
                its = proj_v_items(wslot, 384, vs)
                its += proj_fm_items(wslot, 0, QT[:, qs, :], r_QT[qs])

                def kevac(b, tg):
                    DVE.op(lambda e: e.tensor_scalar(out=KT[:, ks, tg * QG:(tg + 1) * QG], in0=psb[b][:, 0:QG],
                                                     scalar1=mk[:, 0:1], scalar2=None, op0=ALU.mult),
                           [r_ps[b], r_mk], [r_KT[ks]])
                    DVE.op(lambda e: e.tensor_scalar(out=QT[:, k2, tg * QG:(tg + 1) * QG], in0=psb[b][:, 0:QG],
                                                     scalar1=mk[:, 1:2], scalar2=None, op0=ALU.mult),
                           [r_ps[b], r_mk], [r_QT[k2]])
                its += proj_fm_items(wslot, 128, None, None, evac=kevac)
                its += proj_fm_items(wslot, 256, sgT[:, gs, :], r_sg[gs], act_silu=True)
                return its

            for g in range(c.AKV):
                wq = nxt
                nxt = load_w(wb_ev, ti, r_wev[ti]); ti += 1
                mark("A_q")
                a_q_phase(wq)
                wg = nxt
                nxt = load_w(wb_ev, ti, r_wev[ti]); ti += 1
                mark("A_gproj")
                run_items(proj_fm_items(wg, 0, sgT[:, 0, :], r_sg[0], act_silu=True))
                for j in range(c.AG):
                    if j + 1 < c.AG:
                        fl = proj_fm_items(wg, (j + 1) * 128, sgT[:, (j + 1) % 2, :], r_sg[(j + 1) % 2], act_silu=True)
                    elif g == c.AKV - 1:
                        wB = nxt
                        if ti < len(ev_tiles):
                            nxt = load_w(wb_ev, ti, r_wev[ti]); ti += 1
                        fl = b_items(wB, 0)
                    else:
                        fl = ()
                    mark("A_attn")
                    attention(j, g, g, j % 2, 0, g * c.AG + j, "A", 128.0 ** -0.5, filler=fl)
                    yslot += 1
            for h in range(c.BH):
                qs, ks, vs, gs = b_slots(h)
                if h == 0:
                    make_strip(0, slopes_b[0], False)
                pre = strip_gen((h + 1) % 2, slopes_b[h + 1], False) if h + 1 < c.BH else None
                if h + 1 < c.BH:
                    wB = nxt
                    if ti < len(ev_tiles):
                        nxt = load_w(wb_ev, ti, r_wev[ti]); ti += 1
                    fl = b_items(wB, h + 1)
                else:
                    fl = ()
                mark("B_attn")
                attention(qs, ks, vs, gs, 0, c.AH + h, "B", 64.0 ** -0.5, strip_i=h % 2, nmaps=2, filler=fl,
                          k2slot=2 + h % 2, pre=pre)
                yslot += 1
            drain_pending()
            mark("oproj0")
            oproj_phase(wb_oe, r_woe, x, x_res, h1_d, r_h1[s], row0)
            mark("norm1")
            norm_phase(h1_d, r_h1[s], ln_odd_g, row0=row0)
            load_attn_consts(True)
            wC = load_w(wb_od, 0, r_wod[0])
            nxt = load_w(wb_od, 1, r_wod[1])
            mark("C_proj")
            run_items(head_items(wC, 0))
            for h in range(c.CH):
                sl = h % 2
                if h == 0:
                    make_strip(0, slopes_c[0], True)
                pre = strip_gen((h + 1) % 2, slopes_c[h + 1], True) if h + 1 < c.CH else None
                if h + 1 < c.CH:
                    wC = nxt
                    if h + 2 < c.CH:
                        nxt = load_w(wb_od, h + 2, r_wod[h + 2])
                    fl = head_items(wC, (h + 1) % 2)
                else:
                    fl = ()
                mark("C_attn")
                attention(sl, sl, sl, sl, 0, h, "C", 128.0 ** -0.5, strip_i=sl, filler=fl, pre=pre)
                yslot += 1
            drain_pending()
            mark("oproj1")
            oproj_phase(wb_oo, r_woo, h1_d, r_h1[s], h2_d, r_h2[s], row0)
            mark("fnorm")
            norm_phase(h2_d, r_h2[s], final_norm_g, to_out=out, out_res=r_out, row0=row0)
        mark("end")

        for (sem, val) in list(r_out.w.values()):
            POOL.eng.wait_ge(sem, val)

        @block.gpsimd
        def _(e):
            POOL.eng.replay(e)

        @block.sync
        def _(e):
            SP.eng.replay(e)

        @block.scalar
        def _(e):
            ACT.eng.replay(e)

        @block.vector
        def _(e):
            DVE.eng.replay(e)

        @block.tensor
        def _(e):
            PE.eng.replay(e)

        build.stats = {e.key: (e.nins, e.nwaits) for e in (PE, ACT, DVE, POOL, SP)}
        build.nsem = k.nsem
        build.marks = marks
    return nc


def host_consts(cfg):
    S, W, J0 = cfg.S, cfg.W, cfg.J0
    GRID_W, PAIRS = 64, 32
    t = np.arange(S)
    row = (t // GRID_W).astype(np.float32)
    col = (t % GRID_W).astype(np.float32)
    inv = (np.float32(10000.0) ** (-np.arange(PAIRS, dtype=np.float32) / np.float32(PAIRS))).astype(np.float32)
    ar = (row[:, None] * inv[None, :]).astype(np.float32)
    ac = (col[:, None] * inv[None, :]).astype(np.float32)
    cr, sr, cc, sc = np.cos(ar), np.sin(ar), np.cos(ac), np.sin(ac)
    ropeC = np.concatenate([cr, cr, cc, cc], axis=1).astype(np.float32)
    ropeS = np.concatenate([-sr, sr, -sc, sc], axis=1).astype(np.float32)
    p = np.arange(128)[:, None]
    j = np.arange(W)[None, :]
    d = np.abs(j - p - J0)
    dabs = d.astype(np.int16)
    cm = (d <= 64).astype(np.float32) + ((d % 4 == 0) & (d <= 256)).astype(np.float32) \
        + ((d % 16 == 0) & (d <= 1024)).astype(np.float32)
    return {"c_ident": np.eye(128, dtype=np.float32), "c_ropeC": ropeC, "c_ropeS": ropeS,
            "c_dabs": dabs, "c_cmul": cm.astype(np.float32)}


def run(cfg, inputs, trace=False):
    nc = build(cfg)
    consts = host_consts(cfg)
    x = np.ascontiguousarray(np.asarray(inputs["x"], dtype=np.float32))
    B = x.shape[0]
    assert B == N_CORES * cfg.NSEQ
    xs = x.reshape(N_CORES, cfg.NSEQ * cfg.S, cfg.D)
    shared = {}
    for n in ("ln_even_g", "w_in_even", "a_q_norm_g", "a_k_norm_g", "b_lambda_q1", "b_lambda_k1", "b_lambda_q2",
              "b_lambda_k2", "b_subln_g", "w_out_even", "ln_odd_g", "w_in_odd", "w_out_odd"):
        a = np.asarray(inputs[n], dtype=np.float32)
        shared[n] = np.ascontiguousarray(a[0])
    shared["final_norm_g"] = np.ascontiguousarray(np.asarray(inputs["final_norm_g"], dtype=np.float32))
    shared.update(consts)
    in_maps = [dict(shared, x=xs[i]) for i in range(N_CORES)]
    res = run_bass_kernel_spmd(nc, in_maps, core_ids=list(range(N_CORES)), **({"trace": True} if trace else {}))
    outs = [np.asarray(r["out"], dtype=np.float32) for r in res.results]
    full = np.stack(outs, axis=0).reshape(B, cfg.S, cfg.D)
    return full, res


def kernel(**inputs):
    cfg = Cfg()
    full, _ = run(cfg, inputs)
    return full
```
